# Optimizing a Trainium2 kernel written in Bass

```python
import math
import jax, jax.numpy as jnp
from jax import lax
import numpy as np

D_MODEL = 1024
BATCH = 8
SEQ = 8192
DEPTH = 4

N_MEM = 256
EPS = 1e-6
SEQ_WIDTH = 2 * D_MODEL
XA_HEADS = 4
XA_HEAD_DIM = D_MODEL // XA_HEADS
XA_WIDTH = XA_HEADS * XA_HEAD_DIM
MIX_WIDTH = SEQ_WIDTH + XA_WIDTH
GLA_HEADS = 4
GLA_DK = (D_MODEL // 2) // GLA_HEADS
GLA_DV = SEQ_WIDTH // GLA_HEADS
GLA_RANK = 16
GLA_TAU = 16.0
GLA_CHUNK = 64
S5_GROUP = 16
S5_GROUPS = SEQ_WIDTH // S5_GROUP
S5_STATE = 64
S5_CHUNK = 128
S5_DT_MIN = 1e-3
S5_DT_MAX = 1e-1
GLA_IN = 2 * GLA_HEADS * GLA_DK + SEQ_WIDTH + GLA_RANK + MIX_WIDTH + XA_WIDTH
S5_IN = SEQ_WIDTH + MIX_WIDTH + XA_WIDTH
N_GLA = (DEPTH + 1) // 2
N_S5 = DEPTH // 2

kernel_name = "hybrid_gla_s5_memxattn_trunk"


def rms_norm(x, w):
    xf = x.astype(jnp.float32)
    y = xf * lax.rsqrt(jnp.mean(xf * xf, axis=-1, keepdims=True) + EPS)
    return (y * w.astype(jnp.float32)).astype(x.dtype)


def split_cols(t, sizes):
    idx = np.cumsum(sizes)[:-1].tolist()
    return jnp.split(t, idx, axis=-1)


def memory_attention(q, mem_n, w_kv):
    b_, l_, _ = q.shape
    k, v = jnp.split(mem_n @ w_kv, 2, axis=-1)
    qh = q.reshape(b_, l_, XA_HEADS, XA_HEAD_DIM)
    kh = k.reshape(b_, -1, XA_HEADS, XA_HEAD_DIM)
    vh = v.reshape(b_, -1, XA_HEADS, XA_HEAD_DIM)
    s = jnp.einsum('blhd,bmhd->bhlm', qh, kh).astype(jnp.float32) * (XA_HEAD_DIM ** -0.5)
    p = jax.nn.softmax(s, axis=-1).astype(v.dtype)
    o = jnp.einsum('bhlm,bmhd->blhd', p, vh)
    return o.reshape(b_, l_, XA_WIDTH)


def gla_chunked(q, k, v, g):
    b_, l_, h_, dk = q.shape
    dv = v.shape[-1]
    nc = l_ // GLA_CHUNK

    def to_chunks(t):
        return t.reshape(b_, nc, GLA_CHUNK, h_, t.shape[-1]).transpose(1, 0, 3, 2, 4)

    qc, kc, vc, gc = to_chunks(q), to_chunks(k), to_chunks(v), to_chunks(g)
    causal = jnp.tril(jnp.ones((GLA_CHUNK, GLA_CHUNK), dtype=bool))

    def step(state, inp):
        qi, ki, vi, gi = inp
        qf, kf, vf = qi.astype(jnp.float32), ki.astype(jnp.float32), vi.astype(jnp.float32)
        bcum = jnp.cumsum(gi.astype(jnp.float32), axis=-2)
        b_last = bcum[..., -1:, :]
        q_dec = qf * jnp.exp(bcum)
        k_inv = kf * jnp.exp(-bcum)
        k_end = kf * jnp.exp(b_last - bcum)
        att = jnp.where(causal, jnp.einsum('bhcd,bhsd->bhcs', q_dec, k_inv), 0.0)
        o = jnp.einsum('bhcs,bhse->bhce', att, vf) + jnp.einsum('bhcd,bhde->bhce', q_dec, state)
        state = jnp.exp(b_last[..., 0, :])[..., None] * state + jnp.einsum('bhcd,bhce->bhde', k_end, vf)
        return state, o.astype(v.dtype)

    s0 = jnp.zeros((b_, h_, dk, dv), jnp.float32)
    _, o = lax.scan(step, s0, (qc, kc, vc, gc))
    return o.transpose(1, 0, 3, 2, 4).reshape(b_, l_, h_, dv)


def gla_branch(h, w_in, w_gate_up, gate_bias, out_norm_w):
    b_, l_, _ = h.shape
    qk = GLA_HEADS * GLA_DK
    q, k, v, r, z, xq = split_cols(h @ w_in, [qk, qk, SEQ_WIDTH, GLA_RANK, MIX_WIDTH, XA_WIDTH])
    g = jax.nn.log_sigmoid((r @ w_gate_up + gate_bias).astype(jnp.float32)) / GLA_TAU
    o = gla_chunked(q.reshape(b_, l_, GLA_HEADS, GLA_DK) * (GLA_DK ** -0.5),
                    k.reshape(b_, l_, GLA_HEADS, GLA_DK),
                    v.reshape(b_, l_, GLA_HEADS, GLA_DV),
                    g.reshape(b_, l_, GLA_HEADS, GLA_DK))
    o = rms_norm(o, out_norm_w).reshape(b_, l_, SEQ_WIDTH)
    return o.astype(h.dtype), z, xq


def s5_discretize(lam_re, lam_im, log_step, b_re, b_im):
    lam_re = lam_re.astype(jnp.float32)
    lam_im = lam_im.astype(jnp.float32)
    dt = jnp.exp(log_step.astype(jnp.float32))[:, None]
    mag = jnp.exp(lam_re * dt)
    ab_re = mag * jnp.cos(lam_im * dt)
    ab_im = mag * jnp.sin(lam_im * dt)
    den = lam_re * lam_re + lam_im * lam_im
    nr, ni = ab_re - 1.0, ab_im
    coef_re = ((nr * lam_re + ni * lam_im) / den)[..., None]
    coef_im = ((ni * lam_re - nr * lam_im) / den)[..., None]
    br, bi = b_re.astype(jnp.float32), b_im.astype(jnp.float32)
    bb_re = coef_re * br - coef_im * bi
    bb_im = coef_re * bi + coef_im * br
    return ab_re, ab_im, bb_re, bb_im


def _linrec_combine(e1, e2):
    a1r, a1i, x1r, x1i = e1
    a2r, a2i, x2r, x2i = e2
    return (a1r * a2r - a1i * a2i,
            a1r * a2i + a1i * a2r,
            a2r * x1r - a2i * x1i + x2r,
            a2r * x1i + a2i * x1r + x2i)


def s5_chunked(u, ab_re, ab_im, bb_re, bb_im, c_re, c_im):
    b_, l_, g_, hg = u.shape
    nc = l_ // S5_CHUNK
    uc = u.reshape(b_, nc, S5_CHUNK, g_, hg).transpose(1, 0, 2, 3, 4)

    def step(carry, u_i):
        cr, ci = carry
        bu_re = jnp.einsum('bcgh,gph->bcgp', u_i, bb_re)
        bu_im = jnp.einsum('bcgh,gph->bcgp', u_i, bb_im)
        a_r = jnp.broadcast_to(ab_re, bu_re.shape)
        a_i = jnp.broadcast_to(ab_im, bu_im.shape)
        acum_r, acum_i, x_r, x_i = lax.associative_scan(_linrec_combine, (a_r, a_i, bu_re, bu_im), axis=1)
        x_r = x_r + acum_r * cr[:, None] - acum_i * ci[:, None]
        x_i = x_i + acum_r * ci[:, None] + acum_i * cr[:, None]
        y = jnp.einsum('bcgp,ghp->bcgh', x_r, c_re) - jnp.einsum('bcgp,ghp->bcgh', x_i, c_im)
        return (x_r[:, -1], x_i[:, -1]), y

    zeros = jnp.zeros((b_, g_, S5_STATE), jnp.float32)
    _, y = lax.scan(step, (zeros, zeros), uc)
    return y.transpose(1, 0, 2, 3, 4).reshape(b_, l_, g_, hg)


def s5_branch(h, w_in, lam_re, lam_im, log_step, b_re, b_im, c_re, c_im, d, w_glu, b_glu):
    b_, l_, _ = h.shape
    u, z, xq = split_cols(h @ w_in, [SEQ_WIDTH, MIX_WIDTH, XA_WIDTH])
    ab_re, ab_im, bb_re, bb_im = s5_discretize(lam_re, lam_im, log_step, b_re, b_im)
    uf = u.astype(jnp.float32)
    y = s5_chunked(uf.reshape(b_, l_, S5_GROUPS, S5_GROUP), ab_re, ab_im, bb_re, bb_im,
                   c_re.astype(jnp.float32), c_im.astype(jnp.float32))
    y = y.reshape(b_, l_, SEQ_WIDTH) + d.astype(jnp.float32) * uf
    y = jax.nn.gelu(y).astype(h.dtype)
    y = y * jax.nn.sigmoid(y @ w_glu + b_glu)
    return y, z, xq


def setup_inputs(seed: int = 0) -> dict:
    key = jax.random.key(seed)
    ks = jax.random.split(key, 24)
    nrm = jax.random.normal
    inv = lambda n: 1.0 / math.sqrt(n)
    n_idx = jnp.arange(S5_STATE, dtype=jnp.float32)
    lam_re = -0.5 * jnp.exp(0.02 * nrm(ks[8], (N_S5, S5_GROUPS, S5_STATE), jnp.float32))
    lam_im = jnp.broadcast_to(math.pi * n_idx, (N_S5, S5_GROUPS, S5_STATE)).astype(jnp.float32)
    log_step = jax.random.uniform(ks[9], (N_S5, S5_GROUPS), jnp.float32,
                                  math.log(S5_DT_MIN), math.log(S5_DT_MAX))
    return {
        "x": nrm(ks[0], (BATCH, SEQ, D_MODEL), jnp.float32),
        "mem": nrm(ks[1], (BATCH, N_MEM, D_MODEL), jnp.float32),
        "norm_w": 1.0 + 0.02 * nrm(ks[2], (DEPTH, D_MODEL), jnp.float32),
        "mem_norm_w": 1.0 + 0.02 * nrm(ks[3], (D_MODEL,), jnp.float32),
        "gla_w_in": nrm(ks[4], (N_GLA, D_MODEL, GLA_IN), jnp.float32) * inv(D_MODEL),
        "gla_w_gate_up": nrm(ks[5], (N_GLA, GLA_RANK, GLA_HEADS * GLA_DK), jnp.float32) * inv(GLA_RANK),
        "gla_gate_bias": 0.1 * nrm(ks[6], (N_GLA, GLA_HEADS * GLA_DK), jnp.float32),
        "gla_out_norm_w": 1.0 + 0.02 * nrm(ks[7], (N_GLA, GLA_DV), jnp.float32),
        "s5_w_in": nrm(ks[10], (N_S5, D_MODEL, S5_IN), jnp.float32) * inv(D_MODEL),
        "s5_lam_re": lam_re,
        "s5_lam_im": lam_im,
        "s5_log_step": log_step,
        "s5_b_re": nrm(ks[11], (N_S5, S5_GROUPS, S5_STATE, S5_GROUP), jnp.float32) * inv(2 * S5_GROUP),
        "s5_b_im": nrm(ks[12], (N_S5, S5_GROUPS, S5_STATE, S5_GROUP), jnp.float32) * inv(2 * S5_GROUP),
        "s5_c_re": 0.5 * nrm(ks[13], (N_S5, S5_GROUPS, S5_GROUP, S5_STATE), jnp.float32),
        "s5_c_im": 0.5 * nrm(ks[14], (N_S5, S5_GROUPS, S5_GROUP, S5_STATE), jnp.float32),
        "s5_d": nrm(ks[15], (N_S5, SEQ_WIDTH), jnp.float32),
        "s5_w_glu": nrm(ks[16], (N_S5, SEQ_WIDTH, SEQ_WIDTH), jnp.float32) * inv(SEQ_WIDTH),
        "s5_b_glu": 0.01 * nrm(ks[17], (N_S5, SEQ_WIDTH), jnp.float32),
        "xa_w_kv": nrm(ks[18], (DEPTH, D_MODEL, 2 * XA_WIDTH), jnp.float32) * inv(D_MODEL),
        "w_out": nrm(ks[19], (DEPTH, MIX_WIDTH, D_MODEL), jnp.float32) * inv(MIX_WIDTH),
        "final_norm_w": 1.0 + 0.02 * nrm(ks[20], (D_MODEL,), jnp.float32),
    }


def reference(x, mem, norm_w, mem_norm_w, gla_w_in, gla_w_gate_up, gla_gate_bias, gla_out_norm_w,
              s5_w_in, s5_lam_re, s5_lam_im, s5_log_step, s5_b_re, s5_b_im, s5_c_re, s5_c_im,
              s5_d, s5_w_glu, s5_b_glu, xa_w_kv, w_out, final_norm_w):
    mem_n = rms_norm(mem, mem_norm_w)
    for i in range(DEPTH):
        h = rms_norm(x, norm_w[i])
        j = i // 2
        if i % 2 == 0:
            seq_out, z, xq = gla_branch(h, gla_w_in[j], gla_w_gate_up[j], gla_gate_bias[j], gla_out_norm_w[j])
        else:
            seq_out, z, xq = s5_branch(h, s5_w_in[j], s5_lam_re[j], s5_lam_im[j], s5_log_step[j],
                                       s5_b_re[j], s5_b_im[j], s5_c_re[j], s5_c_im[j],
                                       s5_d[j], s5_w_glu[j], s5_b_glu[j])
        xa = memory_attention(xq, mem_n, xa_w_kv[i])
        y = jnp.concatenate([seq_out.astype(x.dtype), xa.astype(x.dtype)], axis=-1) * jax.nn.silu(z)
        x = x + y @ w_out[i]
    return rms_norm(x, final_norm_w)
```

```python
import numpy as np
import concourse.bass as bass
import concourse.mybir as mybir

ENGS = ("pe", "act", "dve", "pool", "sp")
SAME_ENGINE_SYNC = {"pe": False, "act": False, "dve": False, "pool": True, "sp": False}


def _region(ap):
    t = ap.tensor
    name = t.name
    pat = ap.ap
    off = ap.offset
    space = str(ap.space)
    if "DRAM" in space.upper() or "HBM" in space.upper():
        lo = off
        hi = off + sum((c - 1) * abs(s) for s, c in pat) + 1
        return (name, True, 0, 1, lo, hi)
    isz = mybir.dt.size(ap.dtype)
    pstep, pcnt = pat[0]
    if pstep == 0:
        return (name, False, 0, 128, off * isz, (off + sum((c - 1) * abs(s) for s, c in pat[1:]) + 1) * isz)
    p0 = off // pstep
    f0 = (off - p0 * pstep) * isz
    f1 = f0 + (sum((c - 1) * abs(s) for s, c in pat[1:]) + 1) * isz
    return (name, False, p0, p0 + pcnt, f0, f1)


def _overlap(a, b):
    return a[2] < b[3] and b[2] < a[3] and a[4] < b[5] and b[4] < a[5]


def _covers(a, b):
    return a[2] <= b[2] and a[3] >= b[3] and a[4] <= b[4] and a[5] >= b[5]


class Op:
    __slots__ = ("idx", "eng", "fn", "deps", "hard", "is_dma", "dsem", "dcount", "sig", "sigcount", "eseq", "vc")


class Prog:
    def __init__(self, nc):
        self.nc = nc
        self.ops = []
        self.recs = {}
        self.eng_ops = {e: [] for e in ENGS}
        self.dma_sem_of = {}
        self.dma_counts = {}
        self.dma_gen_open = {}
        self.tracked_dram = set()
        self.force_hard = False

    def _deps_for(self, reads, writes, whole_tensor_names=()):
        deps = set()
        for ap_regs, is_w in ((reads, False), (writes, True)):
            for r in ap_regs:
                lst = self.recs.get(r[0])
                if not lst:
                    continue
                whole = r[0] in whole_tensor_names
                for (rr, kind, oi) in lst:
                    if kind == "r" and not is_w:
                        continue
                    if whole or _overlap(r, rr):
                        deps.add(oi)
        return deps

    def _record(self, idx, reads, writes):
        for r in writes:
            lst = self.recs.setdefault(r[0], [])
            lst[:] = [x for x in lst if not _covers(r, x[0])]
            lst.append((r, "w", idx))
        for r in reads:
            lst = self.recs.setdefault(r[0], [])
            eng = self.ops[idx].eng
            lst[:] = [x for x in lst if not (x[1] == "r" and self.ops[x[2]].eng == eng and not self.ops[x[2]].is_dma and not self.ops[idx].is_dma and _covers(r, x[0]))]
            lst.append((r, "r", idx))

    def add(self, eng, fn, reads=(), writes=(), hard=(), soft=False):
        op = Op()
        hr = [_region(a) for a in hard]
        op.hard = self._deps_for(hr, []) if hr else set()
        reads = list(reads) + list(hard)
        op.idx = len(self.ops)
        op.eng = eng
        op.fn = fn
        op.is_dma = False
        op.dsem = None
        op.dcount = 0
        op.sig = False
        rr = [_region(a) for a in reads]
        ww = [_region(a) for a in writes]
        rr = [r for r in rr if not r[1] or r[0] in self.tracked_dram]
        ww = [r for r in ww if not r[1] or r[0] in self.tracked_dram]
        op.deps = self._deps_for(rr, ww)
        if self.force_hard or (not soft and eng in ("act", "dve")):
            op.hard = set(op.deps)
        self.ops.append(op)
        self.eng_ops[eng].append(op.idx)
        self._record(op.idx, rr, ww)
        return op

    def dma(self, eng, out, in_, **kw):
        op = Op()
        op.idx = len(self.ops)
        op.eng = eng
        op.is_dma = True
        op.hard = set()
        op.sig = True
        rr = [_region(in_)]
        ww = [_region(out)]
        sb = [r for r in rr + ww if not r[1]]
        key = sb[0][0] if sb else "dramdram"
        whole = tuple(r[0] for r in sb)
        op.dsem = key
        self.dma_counts[key] = self.dma_counts.get(key, 0) + 1
        op.dcount = self.dma_counts[key]
        op.fn = lambda e, out=out, in_=in_, kw=kw: e.dma_start(out=out, in_=in_, **kw)
        rr = [r for r in rr if not r[1] or r[0] in self.tracked_dram]
        ww = [r for r in ww if not r[1] or r[0] in self.tracked_dram]
        op.deps = self._deps_for(rr, ww, whole)
        self.ops.append(op)
        self.eng_ops[eng].append(op.idx)
        def widen(r):
            if r[1]:
                return r
            return (r[0], False, 0, 128, 0, 1 << 40)
        self._record(op.idx, [widen(r) for r in rr], [widen(r) for r in ww])
        return op

    def finalize(self, sems):
        ops = self.ops
        for e in ENGS:
            for k, oi in enumerate(self.eng_ops[e]):
                ops[oi].eseq = k + 1
        know = {e: {} for e in ENGS}
        selfw = {e: 0 for e in ENGS}
        waits = [None] * len(ops)
        vcs = [None] * len(ops)
        for op in ops:
            K = know[op.eng]
            w = []
            for di in sorted(op.deps, reverse=True):
                d = ops[di]
                if d.is_dma:
                    key = "D:" + d.dsem
                    val = d.dcount
                else:
                    key = d.eng
                    val = d.eseq
                    if d.eng == op.eng and not SAME_ENGINE_SYNC[op.eng] and di not in op.hard:
                        continue
                is_hard_self = (not d.is_dma) and d.eng == op.eng and di in op.hard
                if is_hard_self:
                    if selfw[op.eng] >= val:
                        continue
                    selfw[op.eng] = val
                elif K.get(key, 0) >= val:
                    continue
                w.append(di)
                d.sig = True
                for kk, vv in vcs[di].items():
                    if K.get(kk, 0) < vv:
                        K[kk] = vv
            waits[op.idx] = w
            vc = dict(K)
            if op.is_dma:
                vc["D:" + op.dsem] = op.dcount
            else:
                vc[op.eng] = op.eseq
                K[op.eng] = op.eseq
            vcs[op.idx] = vc
        cnt = {e: 0 for e in ENGS}
        for op in ops:
            if op.is_dma:
                continue
            if op.sig:
                cnt[op.eng] += 1
                op.sigcount = cnt[op.eng]
            else:
                op.sigcount = None
        self.waits = waits
        self.sig_totals = cnt

    EP = 16000

    def n_epochs(self, eng):
        return max(1, -(-self.sig_totals[eng] // self.EP))

    def emit_engine(self, eng, e, sems):
        ops = self.ops
        EP = self.EP
        for oi in self.eng_ops[eng]:
            op = ops[oi]
            need = {}
            for di in self.waits[oi]:
                d = ops[di]
                if d.is_dma:
                    key = "D:" + d.dsem
                    val = 16 * d.dcount
                else:
                    key = d.eng
                    val = d.sigcount
                if need.get(key, 0) < val:
                    need[key] = val
            for key, val in need.items():
                if key.startswith("D:"):
                    e.wait_ge(sems[key], val)
                else:
                    e.wait_ge(sems[key][(val - 1) // EP], (val - 1) % EP + 1)
            ins = op.fn(e)
            if op.is_dma:
                ins.then_inc(sems["D:" + op.dsem], 16)
            elif op.sig:
                ins.then_inc(sems[op.eng][(op.sigcount - 1) // EP], 1)


import math
from contextlib import ExitStack
from concourse.bass_utils import run_bass_kernel_spmd

F32 = mybir.dt.float32
BF16 = mybir.dt.bfloat16
AF = mybir.ActivationFunctionType
ALU = mybir.AluOpType

D = 1024
TT = 256
NB = TT // 128
EPS = 1e-6
GLA_IN = 7184
S5_IN = 6144
TWO_PI = 2.0 * math.pi


def host_consts():
    ident = np.eye(128, dtype=np.float32)
    maskT = (np.arange(128)[None, :] >= np.arange(128)[:, None]).astype(np.float32)
    resetm = np.ones((128, TT), np.float32)
    resetm[:, ::128] = 0.0
    par = (np.arange(128) // 64)
    maskB = np.zeros((128, 4, 8, 16), np.float32)
    for ql in range(4):
        for gl in range(8):
            maskB[:, ql, gl, :] = (gl == 2 * ql + par)[:, None]
    gl_of = np.arange(128) // 16
    maskC = np.zeros((128, 4, 2, 64), np.float32)
    for ql in range(4):
        for pa in range(2):
            maskC[:, ql, pa, :] = (gl_of == 2 * ql + pa)[:, None]
    return {"c_ident": ident, "c_maskT": maskT, "c_resetm": resetm,
            "c_maskB": maskB.reshape(128, -1), "c_maskC": maskC.reshape(128, -1)}


def build(L, depth, Prog):
    nc = bass.Bass("TRN2", target_bir_lowering=False)
    NT = L // TT
    dr = lambda n, s, k="ExternalInput": nc.dram_tensor(n, list(s), F32, kind=k).ap()
    x_in = dr("x", [L, D])
    mem = dr("mem", [256, D])
    norm_w = dr("norm_w", [4, D])
    mem_norm_w = dr("mem_norm_w", [1, D])
    gla_w_in = dr("gla_w_in", [2, D, GLA_IN])
    gla_w_gate_up = dr("gla_w_gate_up", [2, 16, 512])
    gla_gate_bias = dr("gla_gate_bias", [2, 512])
    gla_out_norm_w = dr("gla_out_norm_w", [2, 512])
    s5_w_in = dr("s5_w_in", [2, D, S5_IN])
    s5_lam_re = dr("s5_lam_re", [2, 128, 64])
    s5_lam_im = dr("s5_lam_im", [2, 128, 64])
    s5_log_step = dr("s5_log_step", [2, 128])
    s5_b_re = dr("s5_b_re", [2, 128, 64, 16])
    s5_b_im = dr("s5_b_im", [2, 128, 64, 16])
    s5_c_re = dr("s5_c_re", [2, 128, 16, 64])
    s5_c_im = dr("s5_c_im", [2, 128, 16, 64])
    s5_d = dr("s5_d", [2, 2048])
    s5_w_glu = dr("s5_w_glu", [2, 2048, 2048])
    s5_b_glu = dr("s5_b_glu", [2, 2048])
    xa_w_kv = dr("xa_w_kv", [4, D, 2048])
    w_out = dr("w_out", [4, 3072, D])
    final_norm_w = dr("final_norm_w", [1, D])
    c_ident = dr("c_ident", [128, 128])
    c_maskT = dr("c_maskT", [128, 128])
    c_resetm = dr("c_resetm", [128, TT])
    c_maskB = dr("c_maskB", [128, 512])
    c_maskC = dr("c_maskC", [128, 512])
    out = dr("out", [L, D], "ExternalOutput")
    xs = [dr("xs0", [L, D], "Internal"), dr("xs1", [L, D], "Internal")]
    import os
    DBG = os.environ.get("KDBG") == "1"
    STG = int(os.environ.get("S5STAGE", "99"))
    SUB = int(os.environ.get("S5SUB", "99"))

    class _Skip(Exception):
        pass
    if DBG:
        dbg_h = dr("dbg_h", [128, 8 * TT], "ExternalOutput")
        dbg_y = dr("dbg_y", [128, 24 * TT], "ExternalOutput")
        dbg_q = dr("dbg_q", [128, 4 * TT], "ExternalOutput")
        dbg_k = dr("dbg_k", [128, 4 * TT], "ExternalOutput")
        dbg_bc = dr("dbg_bc", [128, TT], "ExternalOutput")
        dbg_v = dr("dbg_v", [128, NB * 2048], "ExternalOutput")
        dbg_A = dr("dbg_A", [128, 256], "ExternalOutput")
        dbg_u = dr("dbg_u", [128, 16 * TT], "ExternalOutput")
        dbg_yg = dr("dbg_yg", [128, 16 * TT], "ExternalOutput")
        dbg_X = dr("dbg_X", [128, 4096], "ExternalOutput")
        dbg_B = dr("dbg_B", [128, 8192], "ExternalOutput")
        dbg_C = dr("dbg_C", [128, 8192], "ExternalOutput")
        dbg_Bt = dr("dbg_Bt", [128, 2048], "ExternalOutput")
        dbg_Bb = dr("dbg_Bb", [128, 2048], "ExternalOutput")
        dbg_cc = dr("dbg_cc", [128, 128], "ExternalOutput")
    DBGL = int(os.environ.get("KDBGL", "0"))

    P = Prog(nc)
    P.tracked_dram = {"xs0", "xs1"}
    es = ExitStack()
    with es:
        def sb(n, s, d=F32, st=es):
            return st.enter_context(nc.sbuf_tensor(n, list(s), d))

        def psum(n, s, d=F32):
            return es.enter_context(nc.psum_tensor(n, list(s), d))

        def mm(o, l, r, start=True, stop=True):
            P.add("pe", lambda e: e.matmul(o, l, r, start=start, stop=stop), reads=[l, r], writes=[o])

        def tr(o, i, idn):
            P.add("pe", lambda e: e.transpose(o, i, idn), reads=[i, idn], writes=[o])

        def act(o, i, func, bias=None, scale=None, accum=None, eng="act"):
            kw = {}
            rd = [i]
            hd_ = []
            if bias is not None:
                kw["bias"] = bias
                if not isinstance(bias, float):
                    hd_.append(bias)
            if scale is not None:
                kw["scale"] = scale
                if not isinstance(scale, float):
                    hd_.append(scale)
            wr = [o]
            if accum is not None:
                kw["accum_out"] = accum
                wr.append(accum)
            P.add("act", lambda e: e.activation(out=o, in_=i, func=func, **kw), reads=rd, writes=wr, hard=hd_)

        def tt(o, a, b, op, eng="dve", rd=None, wr=None, soft=False):
            P.add(eng, lambda e: e.tensor_tensor(out=o, in0=a, in1=b, op=op), reads=(rd if rd is not None else [a, b]), writes=(wr if wr is not None else [o]), soft=soft)

        def ts(o, a, s1, s2, op0, op1=None, eng="dve"):
            rd = [a]
            hd_ = [s for s in (s1, s2) if s is not None and not isinstance(s, float)]
            if op1 is None:
                P.add(eng, lambda e: e.tensor_scalar(out=o, in0=a, scalar1=s1, scalar2=None, op0=op0), reads=rd, writes=[o], hard=hd_)
            else:
                P.add(eng, lambda e: e.tensor_scalar(out=o, in0=a, scalar1=s1, scalar2=s2, op0=op0, op1=op1), reads=rd, writes=[o], hard=hd_)

        def stt(o, a, s, b, op0, op1, eng="dve"):
            rd = [a, b]
            hd_ = ([] if isinstance(s, float) else [s])
            P.add(eng, lambda e: e.scalar_tensor_tensor(out=o, in0=a, scalar=s, in1=b, op0=op0, op1=op1), reads=rd, writes=[o], hard=hd_)

        def cp(o, i, eng="dve"):
            if eng == "act":
                P.add("act", lambda e: e.activation(out=o, in_=i, func=AF.Copy), reads=[i], writes=[o])
            else:
                P.add(eng, lambda e: e.tensor_copy(out=o, in_=i), reads=[i], writes=[o])

        def recip(o, i):
            P.add("dve", lambda e: e.reciprocal(out=o, in_=i), reads=[i], writes=[o])

        def memset(o, v, eng="dve"):
            P.add(eng, lambda e: e.memset(o, v), reads=[], writes=[o])

        def scan(o, d0, d1, init):
            rd = [d0, d1]
            hd_ = ([] if isinstance(init, float) else [init])
            P.add("dve", lambda e: e.tensor_tensor_scan(out=o, data0=d0, data1=d1, initial=init, op0=ALU.mult, op1=ALU.add), reads=rd, writes=[o], hard=hd_)

        identf = sb("identf", [128, 128])
        identb = sb("identb", [128, 128], BF16)
        onesb = sb("onesb", [128, 128], BF16)
        maskT = sb("maskT", [128, 128])
        resetm = sb("resetm", [128, TT])
        nwbc = sb("nwbc", [128, D])
        fnwbc = sb("fnwbc", [128, D])
        memT = sb("memT", [128, 8, 256], BF16)
        xts = [sb("xt0", [128, NB, D])] * 2
        hTM = sb("hTM", [128, D], BF16)
        hT = sb("hT", [128, 8, TT], BF16)
        wsts = [sb(f"wst{i}", [128, 8, 512], BF16) for i in range(3)]
        ybuf = sb("ybuf", [128, 24, TT], BF16)
        xqT = sb("xqT", [128, 8, TT], BF16)
        kT = sb("kT", [128, 8, 256], BF16)
        vM = sb("vM", [128, 2, 1024], BF16)
        expS = sb("expS", [128, 2, TT], BF16)
        rinv = sb("rinv", [128, TT])
        szb = [sb("sz0", [128, TT], BF16), sb("sz1", [128, TT], BF16)]
        ssq = sb("ssq", [128, 8])
        rstd = sb("rstd", [128, 8])
        junk = sb("junk", [128, D], BF16)
        AW = 30000
        arena = sb("arena", [128, AW])
        aoff = [0]

        def carve(shape, dtype=F32):
            n = 1
            for d_ in shape[1:]:
                n *= d_
            isz = 2 if dtype == BF16 else 4
            words = (n * isz + 3) // 4
            v = arena[:, aoff[0]:aoff[0] + words]
            aoff[0] += words
            assert aoff[0] <= AW, aoff[0]
            if dtype == BF16:
                v = v.bitcast(BF16)[:, 0:n]
            if len(shape) > 2:
                names = "abcd"[:len(shape) - 1]
                kw = {names[i]: shape[1 + i] for i in range(len(shape) - 1)}
                v = v.rearrange("p (" + " ".join(names) + ") -> p " + " ".join(names), **kw)
            if shape[0] < 128:
                v = v[0:shape[0]]
            return v
        psA = [psum(f"psA{i}", [128, 512]) for i in range(6)]
        psT = [psum(f"psT{i}", [128, 1024], BF16) for i in range(2)]
        cnt = {"a": 0, "t": 0, "w": 0, "x": 0, "sz": 0}

        def PA():
            cnt["a"] += 1
            return psA[cnt["a"] % 6]

        def PAt():
            return PA()[:, 0:TT]

        def PT():
            cnt["t"] += 1
            return psT[cnt["t"] % 2]

        def wload(src_ap, ncols=512, nk=8):
            cnt["w"] += 1
            w = wsts[cnt["w"] % 3]
            P.dma("pool", w[:, 0:nk, 0:ncols], src_ap.rearrange("(k p) n -> p k n", p=128))
            return w

        P.dma("sp", identf[:], c_ident)
        P.dma("sp", maskT[:], c_maskT)
        P.dma("sp", resetm[:], c_resetm)
        cp(identb[:], identf[:])
        memset(onesb[:], 1.0)
        P.dma("sp", fnwbc[:], final_norm_w[0:1, :].broadcast_to([128, D]))

        def rmsnorm_block(src, wbc, dst, col):
            memset(ssq[:, col:col + 1], 0.0)
            act(junk[:], src, AF.Square, accum=ssq[:, col:col + 1])
            ts(rstd[:, col:col + 1], ssq[:, col:col + 1], 1.0 / D, EPS, ALU.mult, ALU.add)
            act(rstd[:, col:col + 1], rstd[:, col:col + 1], AF.Sqrt)
            recip(rstd[:, col:col + 1], rstd[:, col:col + 1])
            stt(dst, src, rstd[:, col:col + 1], wbc, ALU.mult, ALU.mult)

        P.dma("sp", nwbc[:], mem_norm_w[0:1, :].broadcast_to([128, D]))
        for mb in range(2):
            xt = xts[mb]
            P.dma("sp", xt[:, 0, :], mem[mb * 128:(mb + 1) * 128, :])
            rmsnorm_block(xt[:, 0, :], nwbc[:], hTM[:], mb)
            pt = PT()
            for kc in range(8):
                tr(pt[:, kc * 128:(kc + 1) * 128], hTM[:, kc * 128:(kc + 1) * 128], identb[:])
            cp(memT[:, :, mb * 128:(mb + 1) * 128], pt[:].rearrange("p (k t) -> p k t", k=8), eng="act")

        for li in range(depth):
            j = li // 2
            is_gla = (li % 2 == 0)
            last = (li == depth - 1)
            src_x = x_in if li == 0 else xs[(li - 1) % 2]
            dst_x = out if last else xs[li % 2]
            w_in = gla_w_in[j] if is_gla else s5_w_in[j]
            ls = ExitStack()
            with ls:
                aoff[0] = 0
                lsb = lambda n, s, d=F32: carve(s, d)
                P.dma("sp", nwbc[:], norm_w[li:li + 1, :].broadcast_to([128, D]))
                for blk in range(4):
                    w = wload(xa_w_kv[li][:, blk * 512:(blk + 1) * 512])
                    if blk < 2:
                        for cs in range(4):
                            pa = PA()
                            for kc in range(8):
                                mm(pa[:, 0:256], w[:, kc, cs * 128:(cs + 1) * 128], memT[:, kc, :], kc == 0, kc == 7)
                            cp(kT[:, blk * 4 + cs, :], pa[:, 0:256], eng="act")
                    else:
                        for mb in range(2):
                            pa = PA()
                            for kc in range(8):
                                mm(pa[:], memT[:, kc, mb * 128:(mb + 1) * 128], w[:, kc, :], kc == 0, kc == 7)
                            cp(vM[:, mb, (blk - 2) * 512:(blk - 1) * 512], pa[:], eng="act")

                if is_gla:
                    wupf = lsb("wupf", [16, 512])
                    wupb = lsb("wupb", [16, 512], BF16)
                    negb = lsb("negb", [128, 4])
                    gnw = lsb("gnw", [128, 512])
                    rT = lsb("rT", [16, TT], BF16)
                    e1 = lsb("e1", [128, TT])
                    spl = lsb("spl", [128, TT])
                    bc = lsb("bc", [128, TT])
                    E1 = lsb("E1", [128, TT])
                    E2 = lsb("E2", [128, TT])
                    E3 = lsb("E3", [128, TT])
                    nbl = lsb("nbl", [128, 4, NB])
                    decay = lsb("decay", [128, 4, NB])
                    qdec = lsb("qdec", [128, 4, TT], BF16)
                    kinv = lsb("kinv", [128, 4, TT], BF16)
                    kend = lsb("kend", [128, 4, TT], BF16)
                    vTM = lsb("vTM", [128, NB, 2048], BF16)
                    stf = lsb("stf", [128, 4, 512])
                    stb = lsb("stb", [128, 4, 512], BF16)
                    attb = lsb("attb", [128, 128], BF16)
                    kendT = lsb("kendT", [128, 128], BF16)
                    onb = lsb("onb", [128, 512], BF16)
                    oss = lsb("oss", [128, 2])
                    P.dma("sp", wupf[:], gla_w_gate_up[j])
                    cp(wupb[:], wupf[:])
                    for hh_ in range(4):
                        P.dma("sp", negb[:, hh_:hh_ + 1], gla_gate_bias[j][hh_ * 128:(hh_ + 1) * 128].rearrange("(p o) -> p o", o=1))
                    ts(negb[:], negb[:], -1.0, None, ALU.mult)
                    P.dma("sp", gnw[:], gla_out_norm_w[j:j + 1, :].broadcast_to([128, 512]))
                    memset(stf[:], 0.0)
                    memset(stb[:], 0.0)
                else:
                    A1 = lsb("A1", [128, 2, 64])
                    AI = lsb("AI", [128, 2, 64])
                    Bpad = lsb("Bpad", [128, 16, 2, 2, 128], BF16)
                    Cpad = lsb("Cpad", [128, 64, 2, 64], BF16)
                    diagD = lsb("diagD", [128, 16, 128], BF16)
                    Dcol = lsb("Dcol", [128, 16])
                    bglu = lsb("bglu", [128, 16])
                    uT = lsb("uT", [128, 16, TT], BF16)
                    yg = lsb("yg", [128, 16, TT], BF16)
                    X = lsb("X", [128, 2, 64, 32])
                    Xb = lsb("Xb", [128, 2, 64, 32], BF16)
                    Xc = lsb("Xc", [128, 2, 64])
                    T1 = lsb("T1", [128, 2, 64])
                    T2 = lsb("T2", [128, 2, 64])
                    sgt = lsb("sgt", [128, TT])
                    for c in range(16):
                        P.dma("sp", Dcol[:, c:c + 1], s5_d[j][c * 128:(c + 1) * 128].rearrange("(p o) -> p o", o=1))
                        P.dma("sp", bglu[:, c:c + 1], s5_b_glu[j][c * 128:(c + 1) * 128].rearrange("(p o) -> p o", o=1))
                    for c in range(16):
                        ts(diagD[:, c, :], identf[:], Dcol[:, c:c + 1], None, ALU.mult)
                    memset(Xc[:], 0.0)
                    ps_ = ExitStack()
                    P.force_hard = True
                    try:
                        psb = lambda n, s, d=F32: carve(s, d)
                        LN = psb("LN", [128, 2, 2, 64])
                        lsbb = psb("lsbb", [128, 128])
                        lre = psb("lre", [128, 64]); lim = psb("lim", [128, 64]); dtt = psb("dtt", [128, 64])
                        t_a = psb("t_a", [128, 64]); t_b = psb("t_b", [128, 64]); t_c = psb("t_c", [128, 64])
                        abr = psb("abr", [128, 64]); abi = psb("abi", [128, 64])
                        cre = psb("cre", [128, 64]); cim = psb("cim", [128, 64])
                        Bt = psb("Bt", [128, 2, 64, 16])
                        Bb = psb("Bb", [128, 2, 64, 16])
                        tB = psb("tB", [128, 64, 16])
                        Cn = psb("Cn", [128, 2, 16, 64])
                        mB = psb("mB", [128, 4, 8, 16]); mC = psb("mC", [128, 4, 2, 64])
                        Zb = [psb("Zb0", [128, 128], BF16), psb("Zb1", [128, 128], BF16)]
                        P.dma("sp", mB[:], c_maskB.rearrange("p (a b c) -> p a b c", a=4, b=8))
                        P.dma("sp", mC[:], c_maskC.rearrange("p (a b c) -> p a b c", a=4, b=2))
                        for wi, srcl in enumerate((s5_lam_re, s5_lam_im)):
                            for dup in range(2):
                                P.dma("sp", LN[:, wi, dup, :], srcl[j])
                        P.dma("sp", lsbb[:], s5_log_step[j:j + 1, :].broadcast_to([128, 128]))
                        if STG < 2: raise _Skip()
                        for wi, dstt in enumerate((lre, lim)):
                            pa = PA()
                            P.add("pe", lambda e, pa=pa, wi=wi: e.transpose(pa[:, 0:128], LN[:, wi, :, :].rearrange("p a b -> p (a b)"), identf[:]),
                                  reads=[LN[:, wi, :, :], identf[:]], writes=[pa[:, 0:128]])
                            for pr in range(2):
                                cp(dstt[pr * 64:(pr + 1) * 64, :], pa[pr * 64:(pr + 1) * 64, pr:128:2])
                        for pr in range(2):
                            cp(dtt[pr * 64:(pr + 1) * 64, :], lsbb[pr * 64:(pr + 1) * 64, pr:128:2])
                        if STG < 3: raise _Skip()
                        act(dtt[:], dtt[:], AF.Exp)
                        tt(t_a[:], lre[:], dtt[:], ALU.mult)
                        act(t_a[:], t_a[:], AF.Exp)
                        tt(t_b[:], lim[:], dtt[:], ALU.mult)
                        kf = psb("kf", [128, 64]); ki = psb("ki", [128, 64]).bitcast(mybir.dt.int32); mk = psb("mk", [128, 64])

                        def sin_of(dst, ang, shift):
                            ts(dst, ang, shift, 1.0 / TWO_PI, ALU.add, ALU.mult)
                            cp(ki[:], dst)
                            cp(kf[:], ki[:])
                            ts(dst, ang, shift, None, ALU.add)
                            stt(dst, kf[:], -TWO_PI, dst, ALU.mult, ALU.add)
                            ts(mk[:], dst, math.pi, None, ALU.is_gt)
                            stt(dst, mk[:], -TWO_PI, dst, ALU.mult, ALU.add)
                            ts(mk[:], dst, -math.pi, None, ALU.is_lt)
                            stt(dst, mk[:], TWO_PI, dst, ALU.mult, ALU.add)
                            act(dst, dst, AF.Sin)
                        sin_of(t_c[:], t_b[:], 0.0)
                        tt(abi[:], t_a[:], t_c[:], ALU.mult)
                        sin_of(t_c[:], t_b[:], 0.5 * math.pi)
                        tt(abr[:], t_a[:], t_c[:], ALU.mult)
                        for ri in range(2):
                            cp(A1[:, ri, :], abr[:])
                        ts(AI[:, 0, :], abi[:], -1.0, None, ALU.mult)
                        cp(AI[:, 1, :], abi[:])
                        tt(t_a[:], lre[:], lre[:], ALU.mult)
                        tt(t_b[:], lim[:], lim[:], ALU.mult)
                        tt(t_a[:], t_a[:], t_b[:], ALU.add)
                        P.add("dve", lambda e: e.reciprocal(out=t_a[:], in_=t_a[:]), reads=[t_a[:]], writes=[t_a[:]])
                        ts(t_b[:], abr[:], -1.0, None, ALU.add)
                        tt(cre[:], t_b[:], lre[:], ALU.mult)
                        tt(t_c[:], abi[:], lim[:], ALU.mult)
                        tt(cre[:], cre[:], t_c[:], ALU.add)
                        tt(cre[:], cre[:], t_a[:], ALU.mult)
                        tt(cim[:], abi[:], lre[:], ALU.mult)
                        tt(t_c[:], t_b[:], lim[:], ALU.mult)
                        tt(cim[:], cim[:], t_c[:], ALU.subtract)
                        tt(cim[:], cim[:], t_a[:], ALU.mult)
                        if STG < 4: raise _Skip()
                        for ri, srcb in enumerate((s5_b_re, s5_b_im)):
                            v = srcb[j].rearrange("(q two) p j -> two p q j", two=2)
                            for pr in range(2):
                                for qq in range(4):
                                    P.dma("sp", Bt[pr * 64:(pr + 1) * 64, ri, qq * 16:(qq + 1) * 16, :], v[pr, :, qq * 16:(qq + 1) * 16, :])
                        creb = cre[:].unsqueeze(2).broadcast_to([128, 64, 16])
                        cimb = cim[:].unsqueeze(2).broadcast_to([128, 64, 16])
                        tt(Bb[:, 0], Bt[:, 0], creb, ALU.mult)
                        tt(tB[:], Bt[:, 1], cimb, ALU.mult)
                        tt(Bb[:, 0], Bb[:, 0], tB[:], ALU.subtract)
                        tt(Bb[:, 1], Bt[:, 1], creb, ALU.mult)
                        tt(tB[:], Bt[:, 0], cimb, ALU.mult)
                        tt(Bb[:, 1], Bb[:, 1], tB[:], ALU.add)
                        if STG < 5: raise _Skip()
                        for ri, srcc in enumerate((s5_c_re, s5_c_im)):
                            v = srcc[j].rearrange("(c gl) i p -> (gl i) c p", gl=8)
                            for hh in range(2):
                                P.dma("sp", Cn[:, ri, hh * 8:(hh + 1) * 8, :], v[:, hh * 8:(hh + 1) * 8, :])
                        if STG < 6: raise _Skip()
                        zi = 0
                        for ri in range(2):
                            for q0 in range(0, 64, 8):
                                ptb = PT()
                                ptc = PT()
                                for qq in range(8):
                                    q = q0 + qq
                                    ql = q % 4
                                    c = q // 4
                                    zb = Zb[zi % 2]; zi += 1
                                    tt(zb[:].rearrange("p (a b) -> p a b", a=8), Bb[:, ri, q:q + 1, :].broadcast_to([128, 8, 16]), mB[:, ql], ALU.mult)
                                    tr(ptb[:, qq * 128:(qq + 1) * 128], zb[:], identb[:])
                                    zc = Zb[zi % 2]; zi += 1
                                    stt(zc[:].rearrange("p (a b) -> p a b", a=2), Cn[:, ri, c:c + 1, :].broadcast_to([128, 2, 64]), (1.0 if ri == 0 else -1.0), mC[:, ql], ALU.mult, ALU.mult)
                                    tr(ptc[:, qq * 128:(qq + 1) * 128], zc[:], identb[:])
                                ptb3 = ptb[:].rearrange("p (a b) -> p a b", a=8)
                                ptc3 = ptc[:].rearrange("p (a b) -> p a b", a=8)
                                for cq in range(2):
                                    c = q0 // 4 + cq
                                    for h in range(2):
                                        qq0 = cq * 4 + 2 * h
                                        cp(Bpad[64 * h:64 * h + 64, c, :, ri, :], ptb3[64 * h:64 * h + 64, qq0:qq0 + 2, :], eng="act")
                                        cp(Cpad[:, q0 + qq0:q0 + qq0 + 2, ri, :], ptc3[:, qq0:qq0 + 2, 64 * h:64 * h + 64], eng="act")
                    except _Skip:
                        pass
                    P.force_hard = False

                for ti in range(NT):
                    tok0 = ti * TT
                    cnt["x"] += 1
                    xt = xts[cnt["x"] % 2]
                    P.dma("sp", xt[:], src_x[tok0:tok0 + TT, :].rearrange("(b p) d -> p b d", p=128))
                    for b in range(NB):
                        rmsnorm_block(xt[:, b, :], nwbc[:], hTM[:], b)
                        pt = PT()
                        for kc in range(8):
                            tr(pt[:, kc * 128:(kc + 1) * 128], hTM[:, kc * 128:(kc + 1) * 128], identb[:])
                        cp(hT[:, :, b * 128:(b + 1) * 128], pt[:].rearrange("p (k t) -> p k t", k=8), eng="act")

                    def proj_fm(w, cs, ncs=1):
                        pa = PAt()
                        for kc in range(8):
                            mm(pa[:], w[:, kc, cs * 128:(cs + 1) * 128], hT[:, kc, :], kc == 0, kc == 7)
                        return pa

                    if is_gla:
                        w = wload(w_in[:, 3072:3088], ncols=16)
                        pa = PAt()
                        for kc in range(8):
                            mm(pa[0:16, :], w[:, kc, 0:16], hT[:, kc, :], kc == 0, kc == 7)
                        cp(rT[:], pa[0:16, :], eng="act")
                        wq = wload(w_in[:, 0:512])
                        wk = wload(w_in[:, 512:1024])
                        for hd in range(4):
                            pa = PAt()
                            mm(pa[:], wupb[:, hd * 128:(hd + 1) * 128], rT[:])
                            act(e1[:], pa[:], AF.Exp, bias=negb[:, hd:hd + 1], scale=-1.0)
                            act(spl[:], e1[:], AF.Ln, bias=1.0)
                            scan(bc[:], resetm[:], spl[:], 0.0)
                            act(E1[:], bc[:], AF.Exp, scale=-1.0 / 16)
                            act(E2[:], bc[:], AF.Exp, scale=1.0 / 16)
                            ts(nbl[:, hd, :], bc[:, 127:TT:128], -1.0 / 16, None, ALU.mult)
                            act(decay[:, hd, :], nbl[:, hd, :], AF.Exp)
                            for b in range(NB):
                                act(E3[:, b * 128:(b + 1) * 128], bc[:, b * 128:(b + 1) * 128], AF.Exp, bias=nbl[:, hd, b:b + 1], scale=1.0 / 16)
                            pq = proj_fm(wq, hd)
                            stt(qdec[:, hd, :], pq[:], 128 ** -0.5, E1[:], ALU.mult, ALU.mult)
                            pk = proj_fm(wk, hd)
                            tt(kinv[:, hd, :], pk[:], E2[:], ALU.mult)
                            tt(kend[:, hd, :], pk[:], E3[:], ALU.mult)
                        for hv in range(4):
                            w = wload(w_in[:, 1024 + hv * 512:1024 + (hv + 1) * 512])
                            for b in range(NB):
                                pa = PA()
                                for kc in range(8):
                                    mm(pa[:], hT[:, kc, b * 128:(b + 1) * 128], w[:, kc, :], kc == 0, kc == 7)
                                cp(vTM[:, b, hv * 512:(hv + 1) * 512], pa[:], eng="act")
                        for b in range(NB):
                            sl = slice(b * 128, (b + 1) * 128)
                            for hd in range(4):
                                pa = PA()
                                mm(pa[:, 0:128], kinv[:, hd, sl], qdec[:, hd, sl])
                                tt(attb[:], pa[:, 0:128], maskT[:], ALU.mult)
                                pt = PT()
                                tr(pt[:, 0:128], kend[:, hd, sl], identb[:])
                                cp(kendT[:], pt[:, 0:128], eng="act")
                                po = PA()
                                mm(po[:], attb[:], vTM[:, b, hd * 512:(hd + 1) * 512], True, False)
                                mm(po[:], qdec[:, hd, sl], stb[:, hd, :], False, True)
                                pst = PA()
                                mm(pst[:], kendT[:], vTM[:, b, hd * 512:(hd + 1) * 512])
                                stt(stf[:, hd, :], stf[:, hd, :], decay[:, hd, b:b + 1], pst[:], ALU.mult, ALU.add)
                                cp(stb[:, hd, :], stf[:, hd, :], eng="pool")
                                memset(oss[:, 0:1], 0.0)
                                act(junk[:, 0:512], po[:], AF.Square, accum=oss[:, 0:1])
                                ts(oss[:, 1:2], oss[:, 0:1], 1.0 / 512, EPS, ALU.mult, ALU.add)
                                act(oss[:, 1:2], oss[:, 1:2], AF.Sqrt)
                                recip(oss[:, 1:2], oss[:, 1:2])
                                stt(onb[:], po[:], oss[:, 1:2], gnw[:], ALU.mult, ALU.mult)
                                pt = PT()
                                for ec in range(4):
                                    tr(pt[:, ec * 128:(ec + 1) * 128], onb[:, ec * 128:(ec + 1) * 128], identb[:])
                                cp(ybuf[:, hd * 4:(hd + 1) * 4, sl], pt[:, 0:512].rearrange("p (a b) -> p a b", a=4), eng="act")
                        zoff, qoff = 3088, 6160
                    else:
                        for ub in range(4):
                            w = wload(w_in[:, ub * 512:(ub + 1) * 512])
                            for cs in range(4):
                                pa = proj_fm(w, cs)
                                cp(uT[:, ub * 4 + cs, :], pa[:], eng="act")
                        for st_ in range(TT // 32 if STG >= 8 else 0):
                            tsl = slice(st_ * 32, (st_ + 1) * 32)
                            for ri in range(2):
                                Xv = X[:, ri].rearrange("p (c l) t -> p c l t", l=4)
                                for chh in range(2):
                                    pah = [PA(), PA()]
                                    for cl in range(8):
                                        c = chh * 8 + cl
                                        for qh in range(2):
                                            for h in range(2):
                                                col = (cl * 2 + qh) * 32
                                                mm(pah[h][:, col:col + 32], Bpad[64 * h:64 * h + 64, c, qh, ri, :], uT[64 * h:64 * h + 64, c, tsl])
                                    if SUB >= 1:
                                        for h in range(2):
                                            cp(Xv[:, chh * 8:chh * 8 + 8, 2 * h:2 * h + 2, :], pah[h][:].rearrange("p (c l t) -> p c l t", c=8, l=2), eng="act")
                            XW = [X[:, :, :, :]]
                            for t in range(32 if STG >= 9 else 0):
                                prev = Xc[:] if t == 0 else X[:, :, :, t - 1]
                                prev_sw0 = Xc[:, 1, :] if t == 0 else X[:, 1, :, t - 1]
                                prev_sw1 = Xc[:, 0, :] if t == 0 else X[:, 0, :, t - 1]
                                sf = (t > 0)
                                tt(T1[:], A1[:], prev, ALU.mult, rd=[A1[:], Xc[:]] + XW, soft=sf)
                                tt(T2[:, 0, :], AI[:, 0, :], prev_sw0, ALU.mult, rd=[AI[:], Xc[:]] + XW, soft=sf)
                                tt(T2[:, 1, :], AI[:, 1, :], prev_sw1, ALU.mult, rd=[AI[:], Xc[:]] + XW, soft=sf)
                                tt(T1[:], T1[:], T2[:], ALU.add, soft=True)
                                tt(X[:, :, :, t], X[:, :, :, t], T1[:], ALU.add, rd=XW + [T1[:]], wr=XW, soft=True)
                            if SUB >= 2:
                                cp(Xc[:], X[:, :, :, 31])
                            if SUB >= 3:
                                cp(Xb[:], X[:], eng="act")
                            if STG < 10:
                                continue
                            pa = PA()
                            for c in range(16):
                                for h in range(2):
                                    osl = pa[64 * h:64 * h + 64, c * 32:(c + 1) * 32]
                                    first = True
                                    for qh in range(2):
                                        q = 4 * c + 2 * h + qh
                                        for ri in range(2):
                                            mm(osl, Cpad[:, q, ri, :], Xb[:, ri, q, :], first, False)
                                            first = False
                                    mm(osl, diagD[:, c, 64 * h:64 * h + 64], uT[:, c, tsl], False, True)
                            act(yg[:, :, tsl], pa[:].rearrange("p (a b) -> p a b", a=16), AF.Gelu)
                        for cg in range(4 if STG >= 11 else 0):
                            pas = [PAt() for _ in range(4)]
                            for kg in range(2):
                                w = wload(s5_w_glu[j][kg * 1024:(kg + 1) * 1024, cg * 512:(cg + 1) * 512])
                                for cs in range(4):
                                    for kc in range(8):
                                        mm(pas[cs][:], w[:, kc, cs * 128:(cs + 1) * 128], yg[:, kg * 8 + kc, :], kg == 0 and kc == 0, kg == 1 and kc == 7)
                            for cs in range(4):
                                c = cg * 4 + cs
                                act(sgt[:], pas[cs][:], AF.Sigmoid, bias=bglu[:, c:c + 1])
                                tt(ybuf[:, c, :], yg[:, c, :], sgt[:], ALU.mult)
                        zoff, qoff = 2048, 5120

                    for qb in range(2):
                        w = wload(w_in[:, qoff + qb * 512:qoff + (qb + 1) * 512])
                        for cs in range(4):
                            pa = proj_fm(w, cs)
                            cp(xqT[:, qb * 4 + cs, :], pa[:], eng="act")
                    for hd in range(4):
                        for mb in range(2):
                            pa = PAt()
                            for jj in range(2):
                                mm(pa[:], kT[:, 2 * hd + jj, mb * 128:(mb + 1) * 128], xqT[:, 2 * hd + jj, :], jj == 0, jj == 1)
                            act(expS[:, mb, :], pa[:], AF.Exp, scale=1.0 / 16)
                        pa = PAt()
                        for mb in range(2):
                            mm(pa[:], onesb[:], expS[:, mb, :], mb == 0, mb == 1)
                        P.add("dve", lambda e, pa=pa: e.reciprocal(out=rinv[:], in_=pa[:]), reads=[pa[:]], writes=[rinv[:]])
                        for jj in range(2):
                            po = PAt()
                            for mb in range(2):
                                mm(po[:], vM[:, mb, (2 * hd + jj) * 128:(2 * hd + jj + 1) * 128], expS[:, mb, :], mb == 0, mb == 1)
                            tt(ybuf[:, 16 + 2 * hd + jj, :], po[:], rinv[:], ALU.mult)
                    for zb in range(6):
                        w = wload(w_in[:, zoff + zb * 512:zoff + (zb + 1) * 512])
                        for cs in range(4):
                            c = zb * 4 + cs
                            pa = proj_fm(w, cs)
                            cnt["sz"] += 1
                            sz = szb[cnt["sz"] % 2]
                            act(sz[:], pa[:], AF.Silu)
                            tt(ybuf[:, c, :], ybuf[:, c, :], sz[:], ALU.mult, eng="pool")
                    if DBG and li == DBGL and ti == 0:
                        P.dma("pool", dbg_h, hT[:].rearrange("p a b -> p (a b)"))
                        P.dma("pool", dbg_y, ybuf[:].rearrange("p a b -> p (a b)"))
                        if is_gla:
                            P.dma("pool", dbg_q, qdec[:].rearrange("p a b -> p (a b)"))
                            P.dma("pool", dbg_k, kinv[:].rearrange("p a b -> p (a b)"))
                            P.dma("sp", dbg_bc, bc[:])
                            P.dma("pool", dbg_v, vTM[:].rearrange("p a b -> p (a b)"))
                        else:
                            P.dma("sp", dbg_A[:, 0:128], A1[:].rearrange("p a b -> p (a b)"))
                            P.dma("sp", dbg_A[:, 128:256], AI[:].rearrange("p a b -> p (a b)"))
                            P.dma("pool", dbg_u, uT[:].rearrange("p a b -> p (a b)"))
                            P.dma("pool", dbg_yg, yg[:].rearrange("p a b -> p (a b)"))
                            P.dma("sp", dbg_X, X[:].rearrange("p a b c -> p (a b c)"))
                            P.dma("pool", dbg_B, Bpad[:].rearrange("p a b c d -> p (a b c d)"))
                            P.dma("pool", dbg_C, Cpad[:].rearrange("p a b c -> p (a b c)"))
                            P.dma("sp", dbg_Bt, Bt[:].rearrange("p a b c -> p (a b c)"))
                            P.dma("sp", dbg_Bb, Bb[:].rearrange("p a b c -> p (a b c)"))
                            P.dma("sp", dbg_cc[:, 0:64], cre[:])
                            P.dma("sp", dbg_cc[:, 64:128], cim[:])
                    for nh in range(2):
                        pas = [PA() for _ in range(NB)]
                        for kg in range(3):
                            w = wload(w_out[li][kg * 1024:(kg + 1) * 1024, nh * 512:(nh + 1) * 512])
                            for b in range(NB):
                                for kc in range(8):
                                    mm(pas[b][:], ybuf[:, kg * 8 + kc, b * 128:(b + 1) * 128], w[:, kc, :], kg == 0 and kc == 0, kg == 2 and kc == 7)
                        for b in range(NB):
                            tt(xt[:, b, nh * 512:(nh + 1) * 512], xt[:, b, nh * 512:(nh + 1) * 512], pas[b][:], ALU.add)
                    if last:
                        for b in range(NB):
                            rmsnorm_block(xt[:, b, :], fnwbc[:], xt[:, b, :], 4 + b)
                    P.dma("sp", dst_x[tok0:tok0 + TT, :].rearrange("(b p) d -> p b d", p=128), xt[:])

        P.finalize(None)
        sems = {}
        for k_ in P.dma_counts:
            sems["D:" + k_] = es.enter_context(nc.semaphore("D_" + k_))
        for en in ("pe", "act", "dve", "pool", "sp"):
            sems[en] = [es.enter_context(nc.semaphore(f"{en}_{i}")) for i in range(P.n_epochs(en))]
        blk = es.enter_context(nc.Block())

        @blk.tensor
        def _(e):
            P.emit_engine("pe", e, sems)

        @blk.scalar
        def _(e):
            P.emit_engine("act", e, sems)

        @blk.vector
        def _(e):
            P.emit_engine("dve", e, sems)

        @blk.gpsimd
        def _(e):
            P.emit_engine("pool", e, sems)

        @blk.sync
        def _(e):
            P.emit_engine("sp", e, sems)
            for k, c in P.dma_counts.items():
                e.wait_ge(sems["D:" + k], 16 * c)
    return nc, P


def kernel(**inputs):
    L = 8192
    nc, P = build(L, 4, Prog)
    consts = host_consts()
    in_maps = []
    for c in range(8):
        m = {}
        for k, v in inputs.items():
            v = np.asarray(v)
            if k == "x":
                m[k] = np.ascontiguousarray(v[c], dtype=np.float32)
            elif k == "mem":
                m[k] = np.ascontiguousarray(v[c], dtype=np.float32)
            elif k in ("mem_norm_w", "final_norm_w"):
                m[k] = np.ascontiguousarray(v.reshape(1, -1), dtype=np.float32)
            else:
                m[k] = np.ascontiguousarray(v, dtype=np.float32)
        m.update(consts)
        in_maps.append(m)
    res = run_bass_kernel_spmd(nc, in_maps, core_ids=list(range(8)))
    return np.stack([r["out"] for r in res.results], axis=0).astype(np.float32)
```

```python
import numpy as np
import concourse.bass as bass
import concourse.mybir as mybir

ENGS = ("pe", "act", "dve", "pool", "sp")
SAME_ENGINE_SYNC = {"pe": False, "act": False, "dve": False, "pool": True, "sp": False}


def _region(ap):
    t = ap.tensor
    name = t.name
    pat = ap.ap
    off = ap.offset
    space = str(ap.space)
    if "DRAM" in space.upper() or "HBM" in space.upper():
        lo = off
        hi = off + sum((c - 1) * abs(s) for s, c in pat) + 1
        return (name, True, 0, 1, lo, hi)
    isz = mybir.dt.size(ap.dtype)
    pstep, pcnt = pat[0]
    if pstep == 0:
        return (name, False, 0, 128, off * isz, (off + sum((c - 1) * abs(s) for s, c in pat[1:]) + 1) * isz)
    p0 = off // pstep
    f0 = (off - p0 * pstep) * isz
    f1 = f0 + (sum((c - 1) * abs(s) for s, c in pat[1:]) + 1) * isz
    return (name, False, p0, p0 + pcnt, f0, f1)


def _overlap(a, b):
    return a[2] < b[3] and b[2] < a[3] and a[4] < b[5] and b[4] < a[5]


def _covers(a, b):
    return a[2] <= b[2] and a[3] >= b[3] and a[4] <= b[4] and a[5] >= b[5]


class Op:
    __slots__ = ("idx", "eng", "fn", "deps", "hard", "is_dma", "dsem", "dcount", "sig", "sigcount", "eseq", "vc")


class Prog:
    def __init__(self, nc):
        self.nc = nc
        self.ops = []
        self.recs = {}
        self.eng_ops = {e: [] for e in ENGS}
        self.dma_sem_of = {}
        self.dma_counts = {}
        self.dma_gen_open = {}
        self.tracked_dram = set()
        self.force_hard = False

    def _deps_for(self, reads, writes, whole_tensor_names=()):
        deps = set()
        for ap_regs, is_w in ((reads, False), (writes, True)):
            for r in ap_regs:
                lst = self.recs.get(r[0])
                if not lst:
                    continue
                whole = r[0] in whole_tensor_names
                for (rr, kind, oi) in lst:
                    if kind == "r" and not is_w:
                        continue
                    if whole or _overlap(r, rr):
                        deps.add(oi)
        return deps

    def _record(self, idx, reads, writes):
        for r in writes:
            lst = self.recs.setdefault(r[0], [])
            lst[:] = [x for x in lst if not _covers(r, x[0])]
            lst.append((r, "w", idx))
        for r in reads:
            lst = self.recs.setdefault(r[0], [])
            eng = self.ops[idx].eng
            lst[:] = [x for x in lst if not (x[1] == "r" and self.ops[x[2]].eng == eng and not self.ops[x[2]].is_dma and not self.ops[idx].is_dma and _covers(r, x[0]))]
            lst.append((r, "r", idx))

    def add(self, eng, fn, reads=(), writes=(), hard=(), soft=False):
        op = Op()
        hr = [_region(a) for a in hard]
        op.hard = self._deps_for(hr, []) if hr else set()
        reads = list(reads) + list(hard)
        op.idx = len(self.ops)
        op.eng = eng
        op.fn = fn
        op.is_dma = False
        op.dsem = None
        op.dcount = 0
        op.sig = False
        rr = [_region(a) for a in reads]
        ww = [_region(a) for a in writes]
        rr = [r for r in rr if not r[1] or r[0] in self.tracked_dram]
        ww = [r for r in ww if not r[1] or r[0] in self.tracked_dram]
        op.deps = self._deps_for(rr, ww)
        if self.force_hard or (not soft and eng in ("act", "dve")):
            op.hard = set(op.deps)
        self.ops.append(op)
        self.eng_ops[eng].append(op.idx)
        self._record(op.idx, rr, ww)
        return op

    def dma(self, eng, out, in_, **kw):
        op = Op()
        op.idx = len(self.ops)
        op.eng = eng
        op.is_dma = True
        op.hard = set()
        op.sig = True
        rr = [_region(in_)]
        ww = [_region(out)]
        sb = [r for r in rr + ww if not r[1]]
        key = sb[0][0] if sb else "dramdram"
        whole = tuple(r[0] for r in sb)
        op.dsem = key
        self.dma_counts[key] = self.dma_counts.get(key, 0) + 1
        op.dcount = self.dma_counts[key]
        op.fn = lambda e, out=out, in_=in_, kw=kw: e.dma_start(out=out, in_=in_, **kw)
        rr = [r for r in rr if not r[1] or r[0] in self.tracked_dram]
        ww = [r for r in ww if not r[1] or r[0] in self.tracked_dram]
        op.deps = self._deps_for(rr, ww, whole)
        self.ops.append(op)
        self.eng_ops[eng].append(op.idx)
        def widen(r):
            if r[1]:
                return r
            return (r[0], False, 0, 128, 0, 1 << 40)
        self._record(op.idx, [widen(r) for r in rr], [widen(r) for r in ww])
        return op

    def finalize(self, sems):
        ops = self.ops
        for e in ENGS:
            for k, oi in enumerate(self.eng_ops[e]):
                ops[oi].eseq = k + 1
        know = {e: {} for e in ENGS}
        selfw = {e: 0 for e in ENGS}
        waits = [None] * len(ops)
        vcs = [None] * len(ops)
        for op in ops:
            K = know[op.eng]
            w = []
            for di in sorted(op.deps, reverse=True):
                d = ops[di]
                if d.is_dma:
                    key = "D:" + d.dsem
                    val = d.dcount
                else:
                    key = d.eng
                    val = d.eseq
                    if d.eng == op.eng and not SAME_ENGINE_SYNC[op.eng] and di not in op.hard:
                        continue
                is_hard_self = (not d.is_dma) and d.eng == op.eng and di in op.hard
                if is_hard_self:
                    if selfw[op.eng] >= val:
                        continue
                    selfw[op.eng] = val
                elif K.get(key, 0) >= val:
                    continue
                w.append(di)
                d.sig = True
                for kk, vv in vcs[di].items():
                    if K.get(kk, 0) < vv:
                        K[kk] = vv
            waits[op.idx] = w
            vc = dict(K)
            if op.is_dma:
                vc["D:" + op.dsem] = op.dcount
            else:
                vc[op.eng] = op.eseq
                K[op.eng] = op.eseq
            vcs[op.idx] = vc
        cnt = {e: 0 for e in ENGS}
        for op in ops:
            if op.is_dma:
                continue
            if op.sig:
                cnt[op.eng] += 1
                op.sigcount = cnt[op.eng]
            else:
                op.sigcount = None
        self.waits = waits
        self.sig_totals = cnt

    EP = 16000

    def n_epochs(self, eng):
        return max(1, -(-self.sig_totals[eng] // self.EP))

    def emit_engine(self, eng, e, sems):
        ops = self.ops
        EP = self.EP
        for oi in self.eng_ops[eng]:
            op = ops[oi]
            need = {}
            for di in self.waits[oi]:
                d = ops[di]
                if d.is_dma:
                    key = "D:" + d.dsem
                    val = 16 * d.dcount
                else:
                    key = d.eng
                    val = d.sigcount
                if need.get(key, 0) < val:
                    need[key] = val
            for key, val in need.items():
                if key.startswith("D:"):
                    e.wait_ge(sems[key], val)
                else:
                    e.wait_ge(sems[key][(val - 1) // EP], (val - 1) % EP + 1)
            ins = op.fn(e)
            if op.is_dma:
                ins.then_inc(sems["D:" + op.dsem], 16)
            elif op.sig:
                ins.then_inc(sems[op.eng][(op.sigcount - 1) // EP], 1)


import math
from contextlib import ExitStack
from concourse.bass_utils import run_bass_kernel_spmd

F32 = mybir.dt.float32
BF16 = mybir.dt.bfloat16
AF = mybir.ActivationFunctionType
ALU = mybir.AluOpType

D = 1024
TT = 256
NB = TT // 128
EPS = 1e-6
GLA_IN = 7184
S5_IN = 6144
TWO_PI = 2.0 * math.pi


def host_consts():
    ident = np.eye(128, dtype=np.float32)
    maskT = (np.arange(128)[None, :] >= np.arange(128)[:, None]).astype(np.float32)
    resetm = np.ones((128, TT), np.float32)
    resetm[:, ::128] = 0.0
    par = (np.arange(128) // 64)
    maskB = np.zeros((128, 4, 8, 16), np.float32)
    for ql in range(4):
        for gl in range(8):
            maskB[:, ql, gl, :] = (gl == 2 * ql + par)[:, None]
    gl_of = np.arange(128) // 16
    maskC = np.zeros((128, 4, 2, 64), np.float32)
    for ql in range(4):
        for pa in range(2):
            maskC[:, ql, pa, :] = (gl_of == 2 * ql + pa)[:, None]
    return {"c_ident": ident, "c_maskT": maskT, "c_resetm": resetm,
            "c_maskB": maskB.reshape(128, -1), "c_maskC": maskC.reshape(128, -1)}


def build(L, depth, Prog):
    nc = bass.Bass("TRN2", target_bir_lowering=False)
    NT = L // TT
    dr = lambda n, s, k="ExternalInput": nc.dram_tensor(n, list(s), F32, kind=k).ap()
    x_in = dr("x", [L, D])
    mem = dr("mem", [256, D])
    norm_w = dr("norm_w", [4, D])
    mem_norm_w = dr("mem_norm_w", [1, D])
    gla_w_in = dr("gla_w_in", [2, D, GLA_IN])
    gla_w_gate_up = dr("gla_w_gate_up", [2, 16, 512])
    gla_gate_bias = dr("gla_gate_bias", [2, 512])
    gla_out_norm_w = dr("gla_out_norm_w", [2, 512])
    s5_w_in = dr("s5_w_in", [2, D, S5_IN])
    s5_lam_re = dr("s5_lam_re", [2, 128, 64])
    s5_lam_im = dr("s5_lam_im", [2, 128, 64])
    s5_log_step = dr("s5_log_step", [2, 128])
    s5_b_re = dr("s5_b_re", [2, 128, 64, 16])
    s5_b_im = dr("s5_b_im", [2, 128, 64, 16])
    s5_c_re = dr("s5_c_re", [2, 128, 16, 64])
    s5_c_im = dr("s5_c_im", [2, 128, 16, 64])
    s5_d = dr("s5_d", [2, 2048])
    s5_w_glu = dr("s5_w_glu", [2, 2048, 2048])
    s5_b_glu = dr("s5_b_glu", [2, 2048])
    xa_w_kv = dr("xa_w_kv", [4, D, 2048])
    w_out = dr("w_out", [4, 3072, D])
    final_norm_w = dr("final_norm_w", [1, D])
    c_ident = dr("c_ident", [128, 128])
    c_maskT = dr("c_maskT", [128, 128])
    c_resetm = dr("c_resetm", [128, TT])
    c_maskB = dr("c_maskB", [128, 512])
    c_maskC = dr("c_maskC", [128, 512])
    out = dr("out", [L, D], "ExternalOutput")
    xs = [dr("xs0", [L, D], "Internal"), dr("xs1", [L, D], "Internal")]
    import os
    DBG = os.environ.get("KDBG") == "1"
    STG = int(os.environ.get("S5STAGE", "99"))
    SUB = int(os.environ.get("S5SUB", "99"))

    class _Skip(Exception):
        pass
    if DBG:
        dbg_h = dr("dbg_h", [128, 8 * TT], "ExternalOutput")
        dbg_y = dr("dbg_y", [128, 24 * TT], "ExternalOutput")
        dbg_q = dr("dbg_q", [128, 4 * TT], "ExternalOutput")
        dbg_k = dr("dbg_k", [128, 4 * TT], "ExternalOutput")
        dbg_bc = dr("dbg_bc", [128, TT], "ExternalOutput")
        dbg_v = dr("dbg_v", [128, NB * 2048], "ExternalOutput")
        dbg_A = dr("dbg_A", [128, 256], "ExternalOutput")
        dbg_u = dr("dbg_u", [128, 16 * TT], "ExternalOutput")
        dbg_yg = dr("dbg_yg", [128, 16 * TT], "ExternalOutput")
        dbg_X = dr("dbg_X", [128, 4096], "ExternalOutput")
        dbg_B = dr("dbg_B", [128, 8192], "ExternalOutput")
        dbg_C = dr("dbg_C", [128, 8192], "ExternalOutput")
        dbg_Bt = dr("dbg_Bt", [128, 2048], "ExternalOutput")
        dbg_Bb = dr("dbg_Bb", [128, 2048], "ExternalOutput")
        dbg_cc = dr("dbg_cc", [128, 128], "ExternalOutput")
    DBGL = int(os.environ.get("KDBGL", "0"))

    P = Prog(nc)
    P.tracked_dram = {"xs0", "xs1"}
    es = ExitStack()
    with es:
        def sb(n, s, d=F32, st=es):
            return st.enter_context(nc.sbuf_tensor(n, list(s), d))

        def psum(n, s, d=F32):
            return es.enter_context(nc.psum_tensor(n, list(s), d))

        def mm(o, l, r, start=True, stop=True):
            P.add("pe", lambda e: e.matmul(o, l, r, start=start, stop=stop), reads=[l, r], writes=[o])

        def tr(o, i, idn):
            P.add("pe", lambda e: e.transpose(o, i, idn), reads=[i, idn], writes=[o])

        def act(o, i, func, bias=None, scale=None, accum=None, eng="act"):
            kw = {}
            rd = [i]
            hd_ = []
            if bias is not None:
                kw["bias"] = bias
                if not isinstance(bias, float):
                    hd_.append(bias)
            if scale is not None:
                kw["scale"] = scale
                if not isinstance(scale, float):
                    hd_.append(scale)
            wr = [o]
            if accum is not None:
                kw["accum_out"] = accum
                wr.append(accum)
            P.add("act", lambda e: e.activation(out=o, in_=i, func=func, **kw), reads=rd, writes=wr, hard=hd_)

        def tt(o, a, b, op, eng="dve", rd=None, wr=None, soft=False):
            P.add(eng, lambda e: e.tensor_tensor(out=o, in0=a, in1=b, op=op), reads=(rd if rd is not None else [a, b]), writes=(wr if wr is not None else [o]), soft=soft)

        def ts(o, a, s1, s2, op0, op1=None, eng="dve"):
            rd = [a]
            hd_ = [s for s in (s1, s2) if s is not None and not isinstance(s, float)]
            if op1 is None:
                P.add(eng, lambda e: e.tensor_scalar(out=o, in0=a, scalar1=s1, scalar2=None, op0=op0), reads=rd, writes=[o], hard=hd_)
            else:
                P.add(eng, lambda e: e.tensor_scalar(out=o, in0=a, scalar1=s1, scalar2=s2, op0=op0, op1=op1), reads=rd, writes=[o], hard=hd_)

        def stt(o, a, s, b, op0, op1, eng="dve"):
            rd = [a, b]
            hd_ = ([] if isinstance(s, float) else [s])
            P.add(eng, lambda e: e.scalar_tensor_tensor(out=o, in0=a, scalar=s, in1=b, op0=op0, op1=op1), reads=rd, writes=[o], hard=hd_)

        def cp(o, i, eng="dve"):
            if eng == "act":
                P.add("act", lambda e: e.activation(out=o, in_=i, func=AF.Copy), reads=[i], writes=[o])
            else:
                P.add(eng, lambda e: e.tensor_copy(out=o, in_=i), reads=[i], writes=[o])

        def recip(o, i):
            P.add("dve", lambda e: e.reciprocal(out=o, in_=i), reads=[i], writes=[o])

        def memset(o, v, eng="dve"):
            P.add(eng, lambda e: e.memset(o, v), reads=[], writes=[o])

        def scan(o, d0, d1, init):
            rd = [d0, d1]
            hd_ = ([] if isinstance(init, float) else [init])
            P.add("dve", lambda e: e.tensor_tensor_scan(out=o, data0=d0, data1=d1, initial=init, op0=ALU.mult, op1=ALU.add), reads=rd, writes=[o], hard=hd_)

        identf = sb("identf", [128, 128])
        identb = sb("identb", [128, 128], BF16)
        onesb = sb("onesb", [128, 128], BF16)
        maskT = sb("maskT", [128, 128])
        resetm = sb("resetm", [128, TT])
        nwbc = sb("nwbc", [128, D])
        fnwbc = sb("fnwbc", [128, D])
        memT = sb("memT", [128, 8, 256], BF16)
        xts = [sb("xt0", [128, NB, D])] * 2
        hTM = sb("hTM", [128, D], BF16)
        hT = sb("hT", [128, 8, TT], BF16)
        wsts = [sb(f"wst{i}", [128, 8, 512], BF16) for i in range(3)]
        ybuf = sb("ybuf", [128, 24, TT], BF16)
        xqT = sb("xqT", [128, 8, TT], BF16)
        kT = sb("kT", [128, 8, 256], BF16)
        vM = sb("vM", [128, 2, 1024], BF16)
        expS = sb("expS", [128, 2, TT], BF16)
        rinv = sb("rinv", [128, TT])
        szb = [sb("sz0", [128, TT], BF16), sb("sz1", [128, TT], BF16)]
        ssq = sb("ssq", [128, 8])
        rstd = sb("rstd", [128, 8])
        junk = sb("junk", [128, D], BF16)
        AW = 30000
        arena = sb("arena", [128, AW])
        aoff = [0]

        def carve(shape, dtype=F32):
            n = 1
            for d_ in shape[1:]:
                n *= d_
            isz = 2 if dtype == BF16 else 4
            words = (n * isz + 3) // 4
            v = arena[:, aoff[0]:aoff[0] + words]
            aoff[0] += words
            assert aoff[0] <= AW, aoff[0]
            if dtype == BF16:
                v = v.bitcast(BF16)[:, 0:n]
            if len(shape) > 2:
                names = "abcd"[:len(shape) - 1]
                kw = {names[i]: shape[1 + i] for i in range(len(shape) - 1)}
                v = v.rearrange("p (" + " ".join(names) + ") -> p " + " ".join(names), **kw)
            if shape[0] < 128:
                v = v[0:shape[0]]
            return v
        psA = [psum(f"psA{i}", [128, 512]) for i in range(6)]
        psT = [psum(f"psT{i}", [128, 1024], BF16) for i in range(2)]
        cnt = {"a": 0, "t": 0, "w": 0, "x": 0, "sz": 0}

        def PA():
            cnt["a"] += 1
            return psA[cnt["a"] % 6]

        def PAt():
            return PA()[:, 0:TT]

        def PT():
            cnt["t"] += 1
            return psT[cnt["t"] % 2]

        def wload(src_ap, ncols=512, nk=8):
            cnt["w"] += 1
            w = wsts[cnt["w"] % 3]
            P.dma("pool", w[:, 0:nk, 0:ncols], src_ap.rearrange("(k p) n -> p k n", p=128))
            return w

        P.dma("sp", identf[:], c_ident)
        P.dma("sp", maskT[:], c_maskT)
        P.dma("sp", resetm[:], c_resetm)
        cp(identb[:], identf[:])
        memset(onesb[:], 1.0)
        P.dma("sp", fnwbc[:], final_norm_w[0:1, :].broadcast_to([128, D]))

        def rmsnorm_block(src, wbc, dst, col):
            memset(ssq[:, col:col + 1], 0.0)
            act(junk[:], src, AF.Square, accum=ssq[:, col:col + 1])
            ts(rstd[:, col:col + 1], ssq[:, col:col + 1], 1.0 / D, EPS, ALU.mult, ALU.add)
            act(rstd[:, col:col + 1], rstd[:, col:col + 1], AF.Sqrt)
            recip(rstd[:, col:col + 1], rstd[:, col:col + 1])
            stt(dst, src, rstd[:, col:col + 1], wbc, ALU.mult, ALU.mult)

        P.dma("sp", nwbc[:], mem_norm_w[0:1, :].broadcast_to([128, D]))
        for mb in range(2):
            xt = xts[mb]
            P.dma("sp", xt[:, 0, :], mem[mb * 128:(mb + 1) * 128, :])
            rmsnorm_block(xt[:, 0, :], nwbc[:], hTM[:], mb)
            pt = PT()
            for kc in range(8):
                tr(pt[:, kc * 128:(kc + 1) * 128], hTM[:, kc * 128:(kc + 1) * 128], identb[:])
            cp(memT[:, :, mb * 128:(mb + 1) * 128], pt[:].rearrange("p (k t) -> p k t", k=8), eng="act")

        for li in range(depth):
            j = li // 2
            is_gla = (li % 2 == 0)
            last = (li == depth - 1)
            src_x = x_in if li == 0 else xs[(li - 1) % 2]
            dst_x = out if last else xs[li % 2]
            w_in = gla_w_in[j] if is_gla else s5_w_in[j]
            ls = ExitStack()
            with ls:
                aoff[0] = 0
                lsb = lambda n, s, d=F32: carve(s, d)
                P.dma("sp", nwbc[:], norm_w[li:li + 1, :].broadcast_to([128, D]))
                for blk in range(4):
                    w = wload(xa_w_kv[li][:, blk * 512:(blk + 1) * 512])
                    if blk < 2:
                        for cs in range(4):
                            pa = PA()
                            for kc in range(8):
                                mm(pa[:, 0:256], w[:, kc, cs * 128:(cs + 1) * 128], memT[:, kc, :], kc == 0, kc == 7)
                            cp(kT[:, blk * 4 + cs, :], pa[:, 0:256], eng="act")
                    else:
                        for mb in range(2):
                            pa = PA()
                            for kc in range(8):
                                mm(pa[:], memT[:, kc, mb * 128:(mb + 1) * 128], w[:, kc, :], kc == 0, kc == 7)
                            cp(vM[:, mb, (blk - 2) * 512:(blk - 1) * 512], pa[:], eng="act")

                if is_gla:
                    wupf = lsb("wupf", [16, 512])
                    wupb = lsb("wupb", [16, 512], BF16)
                    negb = lsb("negb", [128, 4])
                    gnw = lsb("gnw", [128, 512])
                    rT = lsb("rT", [16, TT], BF16)
                    e1 = lsb("e1", [128, TT])
                    spl = lsb("spl", [128, TT])
                    bc = lsb("bc", [128, TT])
                    E1 = lsb("E1", [128, TT])
                    E2 = lsb("E2", [128, TT])
                    E3 = lsb("E3", [128, TT])
                    nbl = lsb("nbl", [128, 4, NB])
                    decay = lsb("decay", [128, 4, NB])
                    qdec = lsb("qdec", [128, 4, TT], BF16)
                    kinv = lsb("kinv", [128, 4, TT], BF16)
                    kend = lsb("kend", [128, 4, TT], BF16)
                    vTM = lsb("vTM", [128, NB, 2048], BF16)
                    stf = lsb("stf", [128, 4, 512])
                    stb = lsb("stb", [128, 4, 512], BF16)
                    attb = lsb("attb", [128, 128], BF16)
                    kendT = lsb("kendT", [128, 128], BF16)
                    onb = lsb("onb", [128, 512], BF16)
                    oss = lsb("oss", [128, 2])
                    P.dma("sp", wupf[:], gla_w_gate_up[j])
                    cp(wupb[:], wupf[:])
                    for hh_ in range(4):
                        P.dma("sp", negb[:, hh_:hh_ + 1], gla_gate_bias[j][hh_ * 128:(hh_ + 1) * 128].rearrange("(p o) -> p o", o=1))
                    ts(negb[:], negb[:], -1.0, None, ALU.mult)
                    P.dma("sp", gnw[:], gla_out_norm_w[j:j + 1, :].broadcast_to([128, 512]))
                    memset(stf[:], 0.0)
                    memset(stb[:], 0.0)
                else:
                    A1 = lsb("A1", [128, 2, 64])
                    AI = lsb("AI", [128, 2, 64])
                    Bpad = lsb("Bpad", [128, 16, 2, 2, 128], BF16)
                    Cpad = lsb("Cpad", [128, 64, 2, 64], BF16)
                    diagD = lsb("diagD", [128, 16, 128], BF16)
                    Dcol = lsb("Dcol", [128, 16])
                    bglu = lsb("bglu", [128, 16])
                    Xc = lsb("Xc", [128, 2, 64])
                    T1 = lsb("T1", [128, 2, 64])
                    T2 = lsb("T2", [128, 2, 64])
                    for c in range(16):
                        P.dma("sp", Dcol[:, c:c + 1], s5_d[j][c * 128:(c + 1) * 128].rearrange("(p o) -> p o", o=1))
                        P.dma("sp", bglu[:, c:c + 1], s5_b_glu[j][c * 128:(c + 1) * 128].rearrange("(p o) -> p o", o=1))
                    for c in range(16):
                        ts(diagD[:, c, :], identf[:], Dcol[:, c:c + 1], None, ALU.mult)
                    memset(Xc[:], 0.0)
                    ps_ = ExitStack()
                    amark = aoff[0]
                    P.force_hard = True
                    try:
                        psb = lambda n, s, d=F32: carve(s, d)
                        LN = psb("LN", [128, 2, 2, 64])
                        lsbb = psb("lsbb", [128, 128])
                        lre = psb("lre", [128, 64]); lim = psb("lim", [128, 64]); dtt = psb("dtt", [128, 64])
                        t_a = psb("t_a", [128, 64]); t_b = psb("t_b", [128, 64]); t_c = psb("t_c", [128, 64])
                        abr = psb("abr", [128, 64]); abi = psb("abi", [128, 64])
                        cre = psb("cre", [128, 64]); cim = psb("cim", [128, 64])
                        Bt = psb("Bt", [128, 2, 64, 16])
                        Bb = psb("Bb", [128, 2, 64, 16])
                        tB = psb("tB", [128, 64, 16])
                        Cn = psb("Cn", [128, 2, 16, 64])
                        mB = psb("mB", [128, 4, 8, 16]); mC = psb("mC", [128, 4, 2, 64])
                        Zb = [psb("Zb0", [128, 128], BF16), psb("Zb1", [128, 128], BF16)]
                        P.dma("sp", mB[:], c_maskB.rearrange("p (a b c) -> p a b c", a=4, b=8))
                        P.dma("sp", mC[:], c_maskC.rearrange("p (a b c) -> p a b c", a=4, b=2))
                        for wi, srcl in enumerate((s5_lam_re, s5_lam_im)):
                            for dup in range(2):
                                P.dma("sp", LN[:, wi, dup, :], srcl[j])
                        P.dma("sp", lsbb[:], s5_log_step[j:j + 1, :].broadcast_to([128, 128]))
                        if STG < 2: raise _Skip()
                        for wi, dstt in enumerate((lre, lim)):
                            pa = PA()
                            P.add("pe", lambda e, pa=pa, wi=wi: e.transpose(pa[:, 0:128], LN[:, wi, :, :].rearrange("p a b -> p (a b)"), identf[:]),
                                  reads=[LN[:, wi, :, :], identf[:]], writes=[pa[:, 0:128]])
                            for pr in range(2):
                                cp(dstt[pr * 64:(pr + 1) * 64, :], pa[pr * 64:(pr + 1) * 64, pr:128:2])
                        for pr in range(2):
                            cp(dtt[pr * 64:(pr + 1) * 64, :], lsbb[pr * 64:(pr + 1) * 64, pr:128:2])
                        if STG < 3: raise _Skip()
                        act(dtt[:], dtt[:], AF.Exp)
                        tt(t_a[:], lre[:], dtt[:], ALU.mult)
                        act(t_a[:], t_a[:], AF.Exp)
                        tt(t_b[:], lim[:], dtt[:], ALU.mult)
                        kf = psb("kf", [128, 64]); ki = psb("ki", [128, 64]).bitcast(mybir.dt.int32); mk = psb("mk", [128, 64])

                        def sin_of(dst, ang, shift):
                            ts(dst, ang, shift, 1.0 / TWO_PI, ALU.add, ALU.mult)
                            cp(ki[:], dst)
                            cp(kf[:], ki[:])
                            ts(dst, ang, shift, None, ALU.add)
                            stt(dst, kf[:], -TWO_PI, dst, ALU.mult, ALU.add)
                            ts(mk[:], dst, math.pi, None, ALU.is_gt)
                            stt(dst, mk[:], -TWO_PI, dst, ALU.mult, ALU.add)
                            ts(mk[:], dst, -math.pi, None, ALU.is_lt)
                            stt(dst, mk[:], TWO_PI, dst, ALU.mult, ALU.add)
                            act(dst, dst, AF.Sin)
                        sin_of(t_c[:], t_b[:], 0.0)
                        tt(abi[:], t_a[:], t_c[:], ALU.mult)
                        sin_of(t_c[:], t_b[:], 0.5 * math.pi)
                        tt(abr[:], t_a[:], t_c[:], ALU.mult)
                        for ri in range(2):
                            cp(A1[:, ri, :], abr[:])
                        ts(AI[:, 0, :], abi[:], -1.0, None, ALU.mult)
                        cp(AI[:, 1, :], abi[:])
                        tt(t_a[:], lre[:], lre[:], ALU.mult)
                        tt(t_b[:], lim[:], lim[:], ALU.mult)
                        tt(t_a[:], t_a[:], t_b[:], ALU.add)
                        P.add("dve", lambda e: e.reciprocal(out=t_a[:], in_=t_a[:]), reads=[t_a[:]], writes=[t_a[:]])
                        ts(t_b[:], abr[:], -1.0, None, ALU.add)
                        tt(cre[:], t_b[:], lre[:], ALU.mult)
                        tt(t_c[:], abi[:], lim[:], ALU.mult)
                        tt(cre[:], cre[:], t_c[:], ALU.add)
                        tt(cre[:], cre[:], t_a[:], ALU.mult)
                        tt(cim[:], abi[:], lre[:], ALU.mult)
                        tt(t_c[:], t_b[:], lim[:], ALU.mult)
                        tt(cim[:], cim[:], t_c[:], ALU.subtract)
                        tt(cim[:], cim[:], t_a[:], ALU.mult)
                        if STG < 4: raise _Skip()
                        for ri, srcb in enumerate((s5_b_re, s5_b_im)):
                            v = srcb[j].rearrange("(q two) p j -> two p q j", two=2)
                            for pr in range(2):
                                for qq in range(4):
                                    P.dma("sp", Bt[pr * 64:(pr + 1) * 64, ri, qq * 16:(qq + 1) * 16, :], v[pr, :, qq * 16:(qq + 1) * 16, :])
                        creb = cre[:].unsqueeze(2).broadcast_to([128, 64, 16])
                        cimb = cim[:].unsqueeze(2).broadcast_to([128, 64, 16])
                        tt(Bb[:, 0], Bt[:, 0], creb, ALU.mult)
                        tt(tB[:], Bt[:, 1], cimb, ALU.mult)
                        tt(Bb[:, 0], Bb[:, 0], tB[:], ALU.subtract)
                        tt(Bb[:, 1], Bt[:, 1], creb, ALU.mult)
                        tt(tB[:], Bt[:, 0], cimb, ALU.mult)
                        tt(Bb[:, 1], Bb[:, 1], tB[:], ALU.add)
                        if STG < 5: raise _Skip()
                        for ri, srcc in enumerate((s5_c_re, s5_c_im)):
                            v = srcc[j].rearrange("(c gl) i p -> (gl i) c p", gl=8)
                            for hh in range(2):
                                P.dma("sp", Cn[:, ri, hh * 8:(hh + 1) * 8, :], v[:, hh * 8:(hh + 1) * 8, :])
                        if STG < 6: raise _Skip()
                        zi = 0
                        for ri in range(2):
                            for q0 in range(0, 64, 8):
                                ptb = PT()
                                ptc = PT()
                                for qq in range(8):
                                    q = q0 + qq
                                    ql = q % 4
                                    c = q // 4
                                    zb = Zb[zi % 2]; zi += 1
                                    tt(zb[:].rearrange("p (a b) -> p a b", a=8), Bb[:, ri, q:q + 1, :].broadcast_to([128, 8, 16]), mB[:, ql], ALU.mult)
                                    tr(ptb[:, qq * 128:(qq + 1) * 128], zb[:], identb[:])
                                    zc = Zb[zi % 2]; zi += 1
                                    stt(zc[:].rearrange("p (a b) -> p a b", a=2), Cn[:, ri, c:c + 1, :].broadcast_to([128, 2, 64]), (1.0 if ri == 0 else -1.0), mC[:, ql], ALU.mult, ALU.mult)
                                    tr(ptc[:, qq * 128:(qq + 1) * 128], zc[:], identb[:])
                                ptb3 = ptb[:].rearrange("p (a b) -> p a b", a=8)
                                ptc3 = ptc[:].rearrange("p (a b) -> p a b", a=8)
                                for cq in range(2):
                                    c = q0 // 4 + cq
                                    for h in range(2):
                                        qq0 = cq * 4 + 2 * h
                                        cp(Bpad[64 * h:64 * h + 64, c, :, ri, :], ptb3[64 * h:64 * h + 64, qq0:qq0 + 2, :], eng="act")
                                        cp(Cpad[:, q0 + qq0:q0 + qq0 + 2, ri, :], ptc3[:, qq0:qq0 + 2, 64 * h:64 * h + 64], eng="act")
                    except _Skip:
                        pass
                    P.force_hard = False
                    aoff[0] = amark
                    uT = lsb("uT", [128, 16, TT], BF16)
                    yg = lsb("yg", [128, 16, TT], BF16)
                    X = lsb("X", [128, 2, 64, 64])
                    Xb = lsb("Xb", [128, 2, 64, 64], BF16)
                    sgt = lsb("sgt", [128, TT])

                for ti in range(NT):
                    tok0 = ti * TT
                    cnt["x"] += 1
                    xt = xts[cnt["x"] % 2]
                    P.dma("sp", xt[:], src_x[tok0:tok0 + TT, :].rearrange("(b p) d -> p b d", p=128))
                    for b in range(NB):
                        rmsnorm_block(xt[:, b, :], nwbc[:], hTM[:], b)
                        pt = PT()
                        for kc in range(8):
                            tr(pt[:, kc * 128:(kc + 1) * 128], hTM[:, kc * 128:(kc + 1) * 128], identb[:])
                        cp(hT[:, :, b * 128:(b + 1) * 128], pt[:].rearrange("p (k t) -> p k t", k=8), eng="act")

                    def proj_fm(w, cs, ncs=1):
                        pa = PAt()
                        for kc in range(8):
                            mm(pa[:], w[:, kc, cs * 128:(cs + 1) * 128], hT[:, kc, :], kc == 0, kc == 7)
                        return pa

                    if is_gla:
                        w = wload(w_in[:, 3072:3088], ncols=16)
                        pa = PAt()
                        for kc in range(8):
                            mm(pa[0:16, :], w[:, kc, 0:16], hT[:, kc, :], kc == 0, kc == 7)
                        cp(rT[:], pa[0:16, :], eng="act")
                        wq = wload(w_in[:, 0:512])
                        wk = wload(w_in[:, 512:1024])
                        for hd in range(4):
                            pa = PAt()
                            mm(pa[:], wupb[:, hd * 128:(hd + 1) * 128], rT[:])
                            act(e1[:], pa[:], AF.Exp, bias=negb[:, hd:hd + 1], scale=-1.0)
                            act(spl[:], e1[:], AF.Ln, bias=1.0)
                            scan(bc[:], resetm[:], spl[:], 0.0)
                            act(E1[:], bc[:], AF.Exp, scale=-1.0 / 16)
                            act(E2[:], bc[:], AF.Exp, scale=1.0 / 16)
                            ts(nbl[:, hd, :], bc[:, 127:TT:128], -1.0 / 16, None, ALU.mult)
                            act(decay[:, hd, :], nbl[:, hd, :], AF.Exp)
                            for b in range(NB):
                                act(E3[:, b * 128:(b + 1) * 128], bc[:, b * 128:(b + 1) * 128], AF.Exp, bias=nbl[:, hd, b:b + 1], scale=1.0 / 16)
                            pq = proj_fm(wq, hd)
                            stt(qdec[:, hd, :], pq[:], 128 ** -0.5, E1[:], ALU.mult, ALU.mult)
                            pk = proj_fm(wk, hd)
                            tt(kinv[:, hd, :], pk[:], E2[:], ALU.mult)
                            tt(kend[:, hd, :], pk[:], E3[:], ALU.mult)
                        for hv in range(4):
                            w = wload(w_in[:, 1024 + hv * 512:1024 + (hv + 1) * 512])
                            for b in range(NB):
                                pa = PA()
                                for kc in range(8):
                                    mm(pa[:], hT[:, kc, b * 128:(b + 1) * 128], w[:, kc, :], kc == 0, kc == 7)
                                cp(vTM[:, b, hv * 512:(hv + 1) * 512], pa[:], eng="act")
                        for b in range(NB):
                            sl = slice(b * 128, (b + 1) * 128)
                            for hd in range(4):
                                pa = PA()
                                mm(pa[:, 0:128], kinv[:, hd, sl], qdec[:, hd, sl])
                                tt(attb[:], pa[:, 0:128], maskT[:], ALU.mult)
                                pt = PT()
                                tr(pt[:, 0:128], kend[:, hd, sl], identb[:])
                                cp(kendT[:], pt[:, 0:128], eng="act")
                                po = PA()
                                mm(po[:], attb[:], vTM[:, b, hd * 512:(hd + 1) * 512], True, False)
                                mm(po[:], qdec[:, hd, sl], stb[:, hd, :], False, True)
                                pst = PA()
                                mm(pst[:], kendT[:], vTM[:, b, hd * 512:(hd + 1) * 512])
                                stt(stf[:, hd, :], stf[:, hd, :], decay[:, hd, b:b + 1], pst[:], ALU.mult, ALU.add)
                                cp(stb[:, hd, :], stf[:, hd, :], eng="pool")
                                memset(oss[:, 0:1], 0.0)
                                act(junk[:, 0:512], po[:], AF.Square, accum=oss[:, 0:1])
                                ts(oss[:, 1:2], oss[:, 0:1], 1.0 / 512, EPS, ALU.mult, ALU.add)
                                act(oss[:, 1:2], oss[:, 1:2], AF.Sqrt)
                                recip(oss[:, 1:2], oss[:, 1:2])
                                stt(onb[:], po[:], oss[:, 1:2], gnw[:], ALU.mult, ALU.mult)
                                pt = PT()
                                for ec in range(4):
                                    tr(pt[:, ec * 128:(ec + 1) * 128], onb[:, ec * 128:(ec + 1) * 128], identb[:])
                                cp(ybuf[:, hd * 4:(hd + 1) * 4, sl], pt[:, 0:512].rearrange("p (a b) -> p a b", a=4), eng="act")
                        zoff, qoff = 3088, 6160
                    else:
                        for ub in range(4):
                            w = wload(w_in[:, ub * 512:(ub + 1) * 512])
                            for cs in range(4):
                                pa = proj_fm(w, cs)
                                cp(uT[:, ub * 4 + cs, :], pa[:], eng="act")
                        for st_ in range(TT // 64):
                            tsl = slice(st_ * 64, (st_ + 1) * 64)
                            for ri in range(2):
                                Xv = X[:, ri].rearrange("p (c l) t -> p c l t", l=4)
                                for cg in range(4):
                                    pah = [PA(), PA()]
                                    for cl in range(4):
                                        c = cg * 4 + cl
                                        for qh in range(2):
                                            for h in range(2):
                                                col = (cl * 2 + qh) * 64
                                                mm(pah[h][:, col:col + 64], Bpad[64 * h:64 * h + 64, c, qh, ri, :], uT[64 * h:64 * h + 64, c, tsl])
                                    for h in range(2):
                                        cp(Xv[:, cg * 4:cg * 4 + 4, 2 * h:2 * h + 2, :], pah[h][:].rearrange("p (c l t) -> p c l t", c=4, l=2), eng="act")
                            XW = [X[:, :, :, :]]
                            for t in range(64):
                                prev = Xc[:] if t == 0 else X[:, :, :, t - 1]
                                prev_sw0 = Xc[:, 1, :] if t == 0 else X[:, 1, :, t - 1]
                                prev_sw1 = Xc[:, 0, :] if t == 0 else X[:, 0, :, t - 1]
                                sf = (t > 0)
                                tt(T1[:], A1[:], prev, ALU.mult, rd=[A1[:], Xc[:]] + XW, soft=sf)
                                tt(T2[:, 0, :], AI[:, 0, :], prev_sw0, ALU.mult, rd=[AI[:], Xc[:]] + XW, soft=sf)
                                tt(T2[:, 1, :], AI[:, 1, :], prev_sw1, ALU.mult, rd=[AI[:], Xc[:]] + XW, soft=sf)
                                tt(T1[:], T1[:], T2[:], ALU.add, soft=True)
                                tt(X[:, :, :, t], X[:, :, :, t], T1[:], ALU.add, rd=XW + [T1[:]], wr=XW, soft=True)
                            cp(Xc[:], X[:, :, :, 63])
                            cp(Xb[:], X[:], eng="act")
                            for chh in range(2):
                                pa = PA()
                                for cl in range(8):
                                    c = chh * 8 + cl
                                    for h in range(2):
                                        osl = pa[64 * h:64 * h + 64, cl * 64:(cl + 1) * 64]
                                        first = True
                                        for qh in range(2):
                                            q = 4 * c + 2 * h + qh
                                            for ri in range(2):
                                                mm(osl, Cpad[:, q, ri, :], Xb[:, ri, q, :], first, False)
                                                first = False
                                        mm(osl, diagD[:, c, 64 * h:64 * h + 64], uT[:, c, tsl], False, True)
                                act(yg[:, chh * 8:chh * 8 + 8, tsl], pa[:].rearrange("p (a b) -> p a b", a=8), AF.Gelu)
                        for cg in range(4):
                            pas = [PAt() for _ in range(4)]
                            for kg in range(2):
                                w = wload(s5_w_glu[j][kg * 1024:(kg + 1) * 1024, cg * 512:(cg + 1) * 512])
                                for cs in range(4):
                                    for kc in range(8):
                                        mm(pas[cs][:], w[:, kc, cs * 128:(cs + 1) * 128], yg[:, kg * 8 + kc, :], kg == 0 and kc == 0, kg == 1 and kc == 7)
                            for cs in range(4):
                                c = cg * 4 + cs
                                act(sgt[:], pas[cs][:], AF.Sigmoid, bias=bglu[:, c:c + 1])
                                tt(ybuf[:, c, :], yg[:, c, :], sgt[:], ALU.mult)
                        zoff, qoff = 2048, 5120

                    for qb in range(2):
                        w = wload(w_in[:, qoff + qb * 512:qoff + (qb + 1) * 512])
                        for cs in range(4):
                            pa = proj_fm(w, cs)
                            cp(xqT[:, qb * 4 + cs, :], pa[:], eng="act")
                    for hd in range(4):
                        for mb in range(2):
                            pa = PAt()
                            for jj in range(2):
                                mm(pa[:], kT[:, 2 * hd + jj, mb * 128:(mb + 1) * 128], xqT[:, 2 * hd + jj, :], jj == 0, jj == 1)
                            act(expS[:, mb, :], pa[:], AF.Exp, scale=1.0 / 16)
                        pa = PAt()
                        for mb in range(2):
                            mm(pa[:], onesb[:], expS[:, mb, :], mb == 0, mb == 1)
                        P.add("dve", lambda e, pa=pa: e.reciprocal(out=rinv[:], in_=pa[:]), reads=[pa[:]], writes=[rinv[:]])
                        for jj in range(2):
                            po = PAt()
                            for mb in range(2):
                                mm(po[:], vM[:, mb, (2 * hd + jj) * 128:(2 * hd + jj + 1) * 128], expS[:, mb, :], mb == 0, mb == 1)
                            tt(ybuf[:, 16 + 2 * hd + jj, :], po[:], rinv[:], ALU.mult)
                    for zb in range(6):
                        w = wload(w_in[:, zoff + zb * 512:zoff + (zb + 1) * 512])
                        for cs in range(4):
                            c = zb * 4 + cs
                            pa = proj_fm(w, cs)
                            cnt["sz"] += 1
                            sz = szb[cnt["sz"] % 2]
                            act(sz[:], pa[:], AF.Silu)
                            tt(ybuf[:, c, :], ybuf[:, c, :], sz[:], ALU.mult, eng="pool")
                    if DBG and li == DBGL and ti == 0:
                        P.dma("pool", dbg_h, hT[:].rearrange("p a b -> p (a b)"))
                        P.dma("pool", dbg_y, ybuf[:].rearrange("p a b -> p (a b)"))
                        if is_gla:
                            P.dma("pool", dbg_q, qdec[:].rearrange("p a b -> p (a b)"))
                            P.dma("pool", dbg_k, kinv[:].rearrange("p a b -> p (a b)"))
                            P.dma("sp", dbg_bc, bc[:])
                            P.dma("pool", dbg_v, vTM[:].rearrange("p a b -> p (a b)"))
                        else:
                            P.dma("sp", dbg_A[:, 0:128], A1[:].rearrange("p a b -> p (a b)"))
                            P.dma("sp", dbg_A[:, 128:256], AI[:].rearrange("p a b -> p (a b)"))
                            P.dma("pool", dbg_u, uT[:].rearrange("p a b -> p (a b)"))
                            P.dma("pool", dbg_yg, yg[:].rearrange("p a b -> p (a b)"))
                            P.dma("sp", dbg_X, X[:].rearrange("p a b c -> p (a b c)"))
                            P.dma("pool", dbg_B, Bpad[:].rearrange("p a b c d -> p (a b c d)"))
                            P.dma("pool", dbg_C, Cpad[:].rearrange("p a b c -> p (a b c)"))
                            P.dma("sp", dbg_Bt, Bt[:].rearrange("p a b c -> p (a b c)"))
                            P.dma("sp", dbg_Bb, Bb[:].rearrange("p a b c -> p (a b c)"))
                            P.dma("sp", dbg_cc[:, 0:64], cre[:])
                            P.dma("sp", dbg_cc[:, 64:128], cim[:])
                    for nh in range(2):
                        pas = [PA() for _ in range(NB)]
                        for kg in range(3):
                            w = wload(w_out[li][kg * 1024:(kg + 1) * 1024, nh * 512:(nh + 1) * 512])
                            for b in range(NB):
                                for kc in range(8):
                                    mm(pas[b][:], ybuf[:, kg * 8 + kc, b * 128:(b + 1) * 128], w[:, kc, :], kg == 0 and kc == 0, kg == 2 and kc == 7)
                        for b in range(NB):
                            tt(xt[:, b, nh * 512:(nh + 1) * 512], xt[:, b, nh * 512:(nh + 1) * 512], pas[b][:], ALU.add)
                    if last:
                        for b in range(NB):
                            rmsnorm_block(xt[:, b, :], fnwbc[:], xt[:, b, :], 4 + b)
                    P.dma("sp", dst_x[tok0:tok0 + TT, :].rearrange("(b p) d -> p b d", p=128), xt[:])

        P.finalize(None)
        sems = {}
        for k_ in P.dma_counts:
            sems["D:" + k_] = es.enter_context(nc.semaphore("D_" + k_))
        for en in ("pe", "act", "dve", "pool", "sp"):
            sems[en] = [es.enter_context(nc.semaphore(f"{en}_{i}")) for i in range(P.n_epochs(en))]
        blk = es.enter_context(nc.Block())

        @blk.tensor
        def _(e):
            P.emit_engine("pe", e, sems)

        @blk.scalar
        def _(e):
            P.emit_engine("act", e, sems)

        @blk.vector
        def _(e):
            P.emit_engine("dve", e, sems)

        @blk.gpsimd
        def _(e):
            P.emit_engine("pool", e, sems)

        @blk.sync
        def _(e):
            P.emit_engine("sp", e, sems)
            for k, c in P.dma_counts.items():
                e.wait_ge(sems["D:" + k], 16 * c)
    return nc, P


def kernel(**inputs):
    L = 8192
    nc, P = build(L, 4, Prog)
    consts = host_consts()
    in_maps = []
    for c in range(8):
        m = {}
        for k, v in inputs.items():
            v = np.asarray(v)
            if k == "x":
                m[k] = np.ascontiguousarray(v[c], dtype=np.float32)
            elif k == "mem":
                m[k] = np.ascontiguousarray(v[c], dtype=np.float32)
            elif k in ("mem_norm_w", "final_norm_w"):
                m[k] = np.ascontiguousarray(v.reshape(1, -1), dtype=np.float32)
            else:
                m[k] = np.ascontiguousarray(v, dtype=np.float32)
        m.update(consts)
        in_maps.append(m)
    res = run_bass_kernel_spmd(nc, in_maps, core_ids=list(range(8)))
    return np.stack([r["out"] for r in res.results], axis=0).astype(np.float32)
```
